# Optimizing a Trainium2 kernel written in Bass

```python
import math
import jax, jax.numpy as jnp
from jax import lax
import numpy as np

D_MODEL = 1024
BATCH = 8
SEQ = 2048
DEPTH = 4
DEC_BATCH = 128
DEC_SEQ = 8
PAST_LEN = 16384
PAGE_SIZE = 128

SSD_HEAD_DIM = 64
SSD_HEADS = D_MODEL // SSD_HEAD_DIM
SSD_INNER = SSD_HEADS * SSD_HEAD_DIM
SSD_GROUPS = 2
SSD_STATE = 128
SSD_CONV = 4
SSD_CHUNK = 128
SSD_CONV_DIM = SSD_INNER + 2 * SSD_GROUPS * SSD_STATE
DT_MIN = 0.001
DT_MAX = 0.1
SGU_WIDTH = D_MODEL // 2
SGU_GROUPS = 4
SGU_CHUNK = 128
SGU_GW = SGU_WIDTH // SGU_GROUPS
POOL_WIDTH = D_MODEL // 2
POOL_WINDOWS = (2, 4, 8, 16)
POOL_GROUPS = len(POOL_WINDOWS)
POOL_GW = POOL_WIDTH // POOL_GROUPS
POOL_BUF = max(POOL_WINDOWS) - 1
D_FF = -(-8 * D_MODEL // (3 * 256)) * 256
PLE_DIM = 256
N_BRANCH = 3
EPS = 1e-6
O_Z = 0
O_XBC = O_Z + SSD_INNER
O_DT = O_XBC + SSD_CONV_DIM
O_UV = O_DT + SSD_HEADS
O_POOL = O_UV + 2 * SGU_WIDTH
O_GATE = O_POOL + POOL_WIDTH
IN_DIM = O_GATE + N_BRANCH * D_MODEL

kernel_name = 'hybrid_ssd_sgu_pool_step'


def rmsnorm(x, g):
    xf = x.astype(jnp.float32)
    y = xf * lax.rsqrt(jnp.mean(xf * xf, axis=-1, keepdims=True) + EPS)
    return (y * g.astype(jnp.float32)).astype(x.dtype)


def segsum(a):
    cs = jnp.cumsum(a, axis=-1)
    diff = cs[..., :, None] - cs[..., None, :]
    T = a.shape[-1]
    mask = jnp.tril(jnp.ones((T, T), dtype=bool))
    return jnp.where(mask, diff, -jnp.inf)


def ssd_scan(xdt, da, b, c, h0):
    n, L, H, P = xdt.shape
    G, N = b.shape[2], b.shape[3]
    R = H // G
    Q = min(SSD_CHUNK, L)
    Lp = -(-L // Q) * Q
    pad = Lp - L
    if pad:
        pw = ((0, 0), (0, pad), (0, 0), (0, 0))
        xdt = jnp.pad(xdt, pw)
        b = jnp.pad(b, pw)
        c = jnp.pad(c, pw)
        da = jnp.pad(da, ((0, 0), (0, pad), (0, 0)))
    nc = Lp // Q
    X = xdt.reshape(n, nc, Q, G, R, P)
    A = da.reshape(n, nc, Q, G, R).transpose(0, 3, 4, 1, 2)
    Bc = b.reshape(n, nc, Q, G, N)
    Cc = c.reshape(n, nc, Q, G, N)
    A_cs = jnp.cumsum(A, axis=-1)
    Lmat = jnp.exp(segsum(A))
    CB = jnp.einsum('bclgn,bcsgn->bgcls', Cc, Bc)
    y_diag = jnp.einsum('bgrcls,bcsgrp->bclgrp', CB[:, :, None] * Lmat, X)
    decay_states = jnp.exp(A_cs[..., -1:] - A_cs)
    states = jnp.einsum('bclgn,bgrcl,bclgrp->bcgrpn', Bc, decay_states, X)
    states = jnp.concatenate([h0.reshape(n, 1, G, R, P, N), states], axis=1)
    chunk_tot = jnp.pad(A_cs[..., -1], ((0, 0), (0, 0), (0, 0), (1, 0)))
    decay_chunk = jnp.exp(segsum(chunk_tot))
    new_states = jnp.einsum('bgrzc,bcgrpn->bzgrpn', decay_chunk, states)
    prev_states, final = new_states[:, :-1], new_states[:, -1]
    y_off = jnp.einsum('bclgn,bcgrpn,bgrcl->bclgrp', Cc, prev_states, jnp.exp(A_cs))
    y = (y_diag + y_off).reshape(n, Lp, H, P)[:, :L]
    return y, final.reshape(n, H, P, N)


def ssd_branch(z, xbc, dt_raw, conv_buf, h0, conv_w, conv_b, dt_bias, a_log, d_skip, norm_g):
    n, L, _ = xbc.shape
    dtype = xbc.dtype
    f32 = jnp.float32
    xp = jnp.concatenate([conv_buf.astype(dtype), xbc], axis=1)
    new_conv = xp[:, -(SSD_CONV - 1):]
    acc = xp[:, 0:L] * conv_w[0]
    for k in range(1, SSD_CONV):
        acc = acc + xp[:, k:k + L] * conv_w[k]
    xbc_c = jax.nn.silu((acc + conv_b).astype(f32))
    xs = xbc_c[..., :SSD_INNER].reshape(n, L, SSD_HEADS, SSD_HEAD_DIM)
    bs = xbc_c[..., SSD_INNER:SSD_INNER + SSD_GROUPS * SSD_STATE].reshape(n, L, SSD_GROUPS, SSD_STATE)
    cs = xbc_c[..., SSD_INNER + SSD_GROUPS * SSD_STATE:].reshape(n, L, SSD_GROUPS, SSD_STATE)
    dt = jax.nn.softplus(dt_raw.astype(f32) + dt_bias.astype(f32))
    a = -jnp.exp(a_log.astype(f32))
    y, h_last = ssd_scan(xs * dt[..., None], dt * a, bs, cs, h0.astype(f32))
    y = y + d_skip.astype(f32)[:, None] * xs
    y = y.reshape(n, L, SSD_INNER) * jax.nn.silu(z.astype(f32))
    yg = y.reshape(n, L, SSD_GROUPS, SSD_INNER // SSD_GROUPS)
    yg = yg * lax.rsqrt(jnp.mean(yg * yg, axis=-1, keepdims=True) + EPS)
    y = yg.reshape(n, L, SSD_INNER) * norm_g.astype(f32)
    return y.astype(dtype), new_conv, h_last.astype(h0.dtype)


def sgu_branch(uv, ln_g, ln_b, w_sp, b_sp):
    n, L, _ = uv.shape
    dtype = uv.dtype
    f32 = jnp.float32
    a = jax.nn.gelu(uv.astype(f32))
    u, v = a[..., :SGU_WIDTH], a[..., SGU_WIDTH:]
    mu = jnp.mean(v, axis=-1, keepdims=True)
    var = jnp.mean(jnp.square(v - mu), axis=-1, keepdims=True)
    vn = (v - mu) * lax.rsqrt(var + EPS) * ln_g.astype(f32) + ln_b.astype(f32)
    Lp = -(-L // SGU_CHUNK) * SGU_CHUNK
    vp = jnp.pad(vn, ((0, 0), (0, Lp - L), (0, 0)))
    vc = vp.reshape(n, Lp // SGU_CHUNK, SGU_CHUNK, SGU_GROUPS, SGU_GW)
    wm = w_sp.astype(f32) * jnp.tril(jnp.ones((SGU_CHUNK, SGU_CHUNK), f32))
    s = jnp.einsum('gts,bcsgd->bctgd', wm, vc) + b_sp.astype(f32).T[None, None, :, :, None]
    s = s.reshape(n, Lp, SGU_WIDTH)[:, :L]
    return (u * s).astype(dtype), vn.astype(dtype)


def pool_branch(xc, buf, start, pool_w, pool_scale):
    n, L, C = xc.shape
    f32 = jnp.float32
    xf = xc.astype(f32)
    xp = jnp.concatenate([buf.astype(f32), xf], axis=1)
    new_buf = xp[:, -POOL_BUF:].astype(xc.dtype)
    cs = jnp.concatenate([jnp.zeros((n, 1, C), f32), jnp.cumsum(xp, axis=1)], axis=1)
    pos = start + jnp.arange(L)
    means = []
    for g, w in enumerate(POOL_WINDOWS):
        sl = slice(g * POOL_GW, (g + 1) * POOL_GW)
        hi = cs[:, POOL_BUF + 1:POOL_BUF + 1 + L, sl]
        lo = cs[:, POOL_BUF + 1 - w:POOL_BUF + 1 - w + L, sl]
        cnt = jnp.minimum(pos + 1, w).astype(f32)
        means.append((hi - lo) / cnt[None, :, None])
    pooled = jnp.concatenate(means, axis=-1)
    d = (pooled - xf).reshape(n, L, POOL_GROUPS, POOL_GW)
    y = jnp.einsum('blgc,gcd->blgd', d, pool_w.astype(f32)).reshape(n, L, C) * pool_scale.astype(f32)
    return y.astype(xc.dtype), new_buf


def trunk(x, p, conv_state, ssm_state, pool_state, start, W):
    (norm_mix, w_in, conv_w, conv_b, dt_bias, a_log, d_skip, ssd_norm, sgu_ln_g, sgu_ln_b,
     w_spatial, b_spatial, pool_w, pool_scale, w_br_a, w_br_b, w_br_c, w_out, norm_ffn,
     w_gate_up, w_down, norm_ple, w_ple_gate, w_ple_up, final_norm) = W
    n, L, _ = x.shape
    v_from = ((start + L - 1) // SGU_CHUNK) * SGU_CHUNK - start
    convs, ssms, pools, vrows = [], [], [], []
    for i in range(DEPTH):
        h = rmsnorm(x, norm_mix[i])
        proj = h @ w_in[i]
        ya, c_new, s_new = ssd_branch(proj[..., O_Z:O_XBC], proj[..., O_XBC:O_DT], proj[..., O_DT:O_UV],
                                      conv_state[i], ssm_state[i], conv_w[i], conv_b[i], dt_bias[i],
                                      a_log[i], d_skip[i], ssd_norm[i])
        yb, v_new = sgu_branch(proj[..., O_UV:O_POOL], sgu_ln_g[i], sgu_ln_b[i], w_spatial[i], b_spatial[i])
        yc, p_new = pool_branch(proj[..., O_POOL:O_GATE], pool_state[i], start, pool_w[i], pool_scale[i])
        g = jax.nn.sigmoid(proj[..., O_GATE:].astype(jnp.float32)).astype(x.dtype)
        merged = (g[..., :D_MODEL] * (ya @ w_br_a[i])
                  + g[..., D_MODEL:2 * D_MODEL] * (yb @ w_br_b[i])
                  + g[..., 2 * D_MODEL:] * (yc @ w_br_c[i]))
        x = x + merged @ w_out[i]
        h = rmsnorm(x, norm_ffn[i])
        gu = h @ w_gate_up[i]
        x = x + (jax.nn.silu(gu[..., :D_FF]) * gu[..., D_FF:]) @ w_down[i]
        h = rmsnorm(x, norm_ple[i])
        x = x + (p[i] @ w_ple_up[i]) * jax.nn.sigmoid(h @ w_ple_gate[i])
        convs.append(c_new)
        ssms.append(s_new)
        pools.append(p_new)
        vrows.append(v_new[:, v_from:])
    y = rmsnorm(x, final_norm)
    return y, jnp.stack(convs), jnp.stack(ssms), jnp.stack(pools), jnp.stack(vrows)


def setup_inputs(seed: int = 0) -> dict:
    key = jax.random.key(seed)
    keys = list(jax.random.split(key, 32))
    f32 = jnp.float32

    def nrm(k, shape, scale):
        return jax.random.normal(k, shape, f32) * scale

    x_prompt = nrm(keys[0], (BATCH, SEQ, D_MODEL), 1.0)
    x_sample = nrm(keys[1], (DEC_BATCH, DEC_SEQ, D_MODEL), 1.0)
    state_conv = nrm(keys[2], (DEPTH, DEC_BATCH, SSD_CONV - 1, SSD_CONV_DIM), 1.0)
    state_ssm = nrm(keys[3], (DEPTH, DEC_BATCH, SSD_HEADS, SSD_HEAD_DIM, SSD_STATE), 0.3)
    state_pool = nrm(keys[4], (DEPTH, DEC_BATCH, POOL_BUF, POOL_WIDTH), 1.0)
    p_prompt = nrm(keys[5], (DEPTH, BATCH, SEQ, PLE_DIM), 1.0)
    p_sample = nrm(keys[6], (DEPTH, DEC_BATCH, DEC_SEQ, PLE_DIM), 1.0)
    norm_mix = 1.0 + nrm(keys[7], (DEPTH, D_MODEL), 0.02)
    w_in = nrm(keys[8], (DEPTH, D_MODEL, IN_DIM), D_MODEL ** -0.5)
    conv_w = nrm(keys[9], (DEPTH, SSD_CONV, SSD_CONV_DIM), SSD_CONV ** -0.5)
    conv_b = nrm(keys[10], (DEPTH, SSD_CONV_DIM), 0.02)
    dt0 = jnp.exp(jax.random.uniform(keys[11], (DEPTH, SSD_HEADS), f32, math.log(DT_MIN), math.log(DT_MAX)))
    dt_bias = dt0 + jnp.log(-jnp.expm1(-dt0))
    a_log = jnp.log(jax.random.uniform(keys[12], (DEPTH, SSD_HEADS), f32, 1.0, 16.0))
    d_skip = 1.0 + nrm(keys[13], (DEPTH, SSD_HEADS), 0.02)
    ssd_norm = 1.0 + nrm(keys[14], (DEPTH, SSD_INNER), 0.02)
    sgu_ln_g = 1.0 + nrm(keys[15], (DEPTH, SGU_WIDTH), 0.02)
    sgu_ln_b = nrm(keys[16], (DEPTH, SGU_WIDTH), 0.02)
    w_spatial = nrm(keys[17], (DEPTH, SGU_GROUPS, SGU_CHUNK, SGU_CHUNK), SGU_CHUNK ** -0.5)
    b_spatial = 1.0 + nrm(keys[18], (DEPTH, SGU_GROUPS, SGU_CHUNK), 0.02)
    pool_w = nrm(keys[19], (DEPTH, POOL_GROUPS, POOL_GW, POOL_GW), POOL_GW ** -0.5)
    pool_scale = 1.0 + nrm(keys[20], (DEPTH, POOL_WIDTH), 0.02)
    w_br_a = nrm(keys[21], (DEPTH, SSD_INNER, D_MODEL), SSD_INNER ** -0.5)
    w_br_b = nrm(keys[22], (DEPTH, SGU_WIDTH, D_MODEL), SGU_WIDTH ** -0.5)
    w_br_c = nrm(keys[23], (DEPTH, POOL_WIDTH, D_MODEL), POOL_WIDTH ** -0.5)
    w_out = nrm(keys[24], (DEPTH, D_MODEL, D_MODEL), D_MODEL ** -0.5)
    norm_ffn = 1.0 + nrm(keys[25], (DEPTH, D_MODEL), 0.02)
    w_gate_up = nrm(keys[26], (DEPTH, D_MODEL, 2 * D_FF), D_MODEL ** -0.5)
    w_down = nrm(keys[27], (DEPTH, D_FF, D_MODEL), D_FF ** -0.5)
    norm_ple = 1.0 + nrm(keys[28], (DEPTH, D_MODEL), 0.02)
    w_ple_gate = nrm(keys[29], (DEPTH, D_MODEL, D_MODEL), D_MODEL ** -0.5)
    w_ple_up = nrm(keys[30], (DEPTH, PLE_DIM, D_MODEL), PLE_DIM ** -0.5)
    final_norm = 1.0 + nrm(keys[31], (D_MODEL,), 0.02)
    return {'x_prompt': x_prompt, 'x_sample': x_sample, 'state_conv': state_conv, 'state_ssm': state_ssm,
            'state_pool': state_pool, 'p_prompt': p_prompt, 'p_sample': p_sample, 'norm_mix': norm_mix,
            'w_in': w_in, 'conv_w': conv_w, 'conv_b': conv_b, 'dt_bias': dt_bias, 'a_log': a_log,
            'd_skip': d_skip, 'ssd_norm': ssd_norm, 'sgu_ln_g': sgu_ln_g, 'sgu_ln_b': sgu_ln_b,
            'w_spatial': w_spatial, 'b_spatial': b_spatial, 'pool_w': pool_w, 'pool_scale': pool_scale,
            'w_br_a': w_br_a, 'w_br_b': w_br_b, 'w_br_c': w_br_c, 'w_out': w_out, 'norm_ffn': norm_ffn,
            'w_gate_up': w_gate_up, 'w_down': w_down, 'norm_ple': norm_ple, 'w_ple_gate': w_ple_gate,
            'w_ple_up': w_ple_up, 'final_norm': final_norm}


def reference(x_prompt, x_sample, state_conv, state_ssm, state_pool, p_prompt, p_sample, norm_mix, w_in,
              conv_w, conv_b, dt_bias, a_log, d_skip, ssd_norm, sgu_ln_g, sgu_ln_b, w_spatial, b_spatial,
              pool_w, pool_scale, w_br_a, w_br_b, w_br_c, w_out, norm_ffn, w_gate_up, w_down, norm_ple,
              w_ple_gate, w_ple_up, final_norm):
    W = (norm_mix, w_in, conv_w, conv_b, dt_bias, a_log, d_skip, ssd_norm, sgu_ln_g, sgu_ln_b,
         w_spatial, b_spatial, pool_w, pool_scale, w_br_a, w_br_b, w_br_c, w_out, norm_ffn,
         w_gate_up, w_down, norm_ple, w_ple_gate, w_ple_up, final_norm)
    dt = x_prompt.dtype
    zero_conv = jnp.zeros((DEPTH, BATCH, SSD_CONV - 1, SSD_CONV_DIM), dt)
    zero_ssm = jnp.zeros((DEPTH, BATCH, SSD_HEADS, SSD_HEAD_DIM, SSD_STATE), dt)
    zero_pool = jnp.zeros((DEPTH, BATCH, POOL_BUF, POOL_WIDTH), dt)
    y_prompt, conv_p, ssm_p, pool_p, v_p = trunk(x_prompt, p_prompt, zero_conv, zero_ssm, zero_pool, 0, W)
    y_sample, conv_s, ssm_s, pool_s, v_s = trunk(x_sample, p_sample, state_conv, state_ssm, state_pool,
                                                 PAST_LEN, W)
    return (y_prompt, y_sample, conv_p, ssm_p, pool_p, v_p, conv_s, ssm_s, pool_s, v_s)
```

```python
import math
import numpy as np
import concourse.bass as bass
import concourse.mybir as mybir
from concourse.bass_utils import run_bass_kernel_spmd

F32 = mybir.dt.float32
BF16 = mybir.dt.bfloat16
AF = mybir.ActivationFunctionType
ALU = mybir.AluOpType

NCORES = 8
DEPTH = 4
D = 1024
NTILE = 17
TOK = NTILE * 128
D_FF = 2816
O_Z, O_XBC, O_DT, O_UV, O_POOL, O_GATE = 0, 1024, 2560, 2576, 3600, 4112
IN_DIM = 7184
EPS = 1e-6
GROUPS = [(0, 4, False), (4, 4, False), (8, 4, False), (12, 4, False), (16, 1, True)]
RUN_DEPTH = DEPTH
STAGES = 99
DEBUG_CORES = 0
DEBUG_TRACE = False
SSDLVL = 99
PIPELINE = True
FAST_PERHEAD = True
FAST_GELU = True

ENGS = ("pe", "act", "dve", "pool", "sp")


class Buf:
    __slots__ = ("name", "w", "r", "dsem", "dcnt", "excl")

    def __init__(self, name, excl=False):
        self.name = name
        self.w = None
        self.r = {}
        self.dsem = None
        self.dcnt = 0
        self.excl = excl


class Op:
    __slots__ = ("eng", "fn", "deps", "needed", "tok", "is_dma", "dsem")

    def __init__(self, eng, fn):
        self.eng = eng
        self.fn = fn
        self.deps = []
        self.needed = False
        self.tok = None
        self.is_dma = False
        self.dsem = None


class Sched:
    def __init__(self, nc):
        self.nc = nc
        self.ops = {e: [] for e in ENGS}
        self.all_ops = []
        self.dma_bufs = []
        self.nsem = 0
        self.last = {e: None for e in ENGS}
        self.pending = {e: [] for e in ENGS}

    def new_sem(self, name):
        self.nsem += 1
        return self.nc.alloc_semaphore(name)

    def barrier(self, engines=("pe", "act", "dve")):
        for e in engines:
            for o in engines:
                if o != e and self.last[o] is not None:
                    self.pending[e].append(self.last[o])

    def _track(self, rec, reads, writes):
        deps = []
        for b in reads:
            if b.w is not None:
                deps.append(b.w)
            if b.excl:
                deps.extend(v for k_, v in b.r.items() if k_ != rec.eng)
        for b in writes:
            if b.w is not None:
                deps.append(b.w)
            deps.extend(b.r.values())
        if self.pending[rec.eng]:
            deps.extend(self.pending[rec.eng])
            self.pending[rec.eng] = []
        seen = set()
        out = []
        for d in deps:
            if d is rec or id(d) in seen:
                continue
            if rec.eng == "pe" and d.eng == "pe" and not d.is_dma and not rec.is_dma:
                continue
            seen.add(id(d))
            out.append(d)
            d.needed = True
        rec.deps = out
        key = rec.eng + ("_dma" if rec.is_dma else "")
        for b in reads:
            b.r[key] = rec
        for b in writes:
            b.w = rec
            b.r = {}
        self.ops[rec.eng].append(rec)
        self.all_ops.append(rec)
        if not rec.is_dma:
            self.last[rec.eng] = rec

    def op(self, eng, fn, reads=(), writes=()):
        rec = Op(eng, fn)
        self._track(rec, reads, writes)
        return rec

    def dma(self, eng, out_ap, in_ap, sbuf, is_load, extra_reads=(), cont=False, **kw):
        rec = Op(eng, lambda e: e.dma_start(out=out_ap, in_=in_ap, **kw))
        rec.is_dma = True
        if sbuf.dsem is None:
            sbuf.dsem = self.new_sem("d_" + sbuf.name)
            self.dma_bufs.append(sbuf)
        sbuf.dcnt += 16
        rec.dsem = sbuf.dsem
        rec.tok = (sbuf.dsem, sbuf.dcnt)
        if is_load and cont:
            assert sbuf.w is not None and sbuf.w.is_dma and sbuf.w.eng == eng and not sbuf.r
            rec.deps = []
            sbuf.w = rec
            self.ops[eng].append(rec)
            self.all_ops.append(rec)
        elif is_load:
            self._track(rec, list(extra_reads), [sbuf])
        else:
            self._track(rec, [sbuf] + list(extra_reads), [])
        return rec

    def emit(self):
        nc = self.nc
        esem = {e: self.new_sem("e_" + e) for e in ENGS}
        cnt = {e: 0 for e in ENGS}
        for rec in self.all_ops:
            if rec.is_dma:
                continue
            if rec.needed:
                cnt[rec.eng] += 1
                rec.tok = (esem[rec.eng], cnt[rec.eng])
        final_waits = [(b.dsem, b.dcnt) for b in self.dma_bufs]
        ops = self.ops

        def run(eng_name, e):
            waited = {}
            for rec in ops[eng_name]:
                for d in rec.deps:
                    sem, val = d.tok
                    k = sem.num
                    if waited.get(k, 0) < val:
                        e.wait_ge(sem, val)
                        waited[k] = val
                ins = rec.fn(e)
                if rec.is_dma:
                    ins.then_inc(rec.dsem, 16)
                elif rec.needed:
                    ins.then_inc(esem[eng_name], 1)
            if eng_name == "sp":
                for sem, val in final_waits:
                    if waited.get(sem.num, 0) < val:
                        e.wait_ge(sem, val)

        with nc.Block() as block:
            @block.tensor
            def _(e):
                run("pe", e)

            @block.scalar
            def _(e):
                run("act", e)

            @block.vector
            def _(e):
                run("dve", e)

            @block.gpsimd
            def _(e):
                run("pool", e)

            @block.sync
            def _(e):
                run("sp", e)
        print("sched: ops", {k: len(v) for k, v in ops.items()}, "signals", cnt, "nsem", self.nsem, flush=True)


C_ID, C_ONES, C_U, C_NEGU, C_CM, C_US, C_OS, C_IND = range(8)
C_DCUR0, C_DCUR, C_DPREV, C_DCURS, C_DPA, C_DPB = 8, 12, 16, 20, 24, 28
NCST = 32
POOL_W = (2, 4, 8, 16)


def make_consts():
    c = np.zeros((NCST, 128, 128), np.float32)
    k = np.arange(128)[:, None]
    l = np.arange(128)[None, :]
    c[C_ID] = (k == l)
    c[C_ONES] = 1.0
    c[C_U] = (k <= l)
    c[C_NEGU] = -(k <= l).astype(np.float32)
    c[C_CM] = 3.0e4 * (l < k)
    same = (k // 8) == (l // 8)
    c[C_US] = (k <= l) & same
    c[C_OS] = same
    c[C_IND][:, :16] = (k // 8) == np.arange(16)[None, :]
    s, t = k, l
    for g, w in enumerate(POOL_W):
        band = ((s <= t) & (s > t - w)).astype(np.float32)
        cnt0 = np.minimum(np.arange(128) + 1, w).astype(np.float32)[None, :]
        c[C_DCUR0 + g] = band / cnt0 - (s == t)
        c[C_DCUR + g] = band / w - (s == t)
        c[C_DPREV + g] = (s > 128 + t - w).astype(np.float32) / w
        si, ti = s % 8, t % 8
        c[C_DCURS + g] = (same & (si <= ti) & (si > ti - w)).astype(np.float32) / w - (s == t)
        for half, idx in ((0, C_DPA), (1, C_DPB)):
            m = np.zeros((128, 128), np.float32)
            for row in range(120):
                j = row // 15 + 8 * half
                r = row % 15
                for col in range(128):
                    if col // 8 == j and r > 15 + (col % 8) - w:
                        m[row, col] = 1.0 / w
            c[idx + g] = m
    return c


def build_program():
    nc = bass.Bass("TRN2", target_bir_lowering=False)
    with nc.cleanup_on_exit():
        _build_body(nc)
        nc.all_engine_barrier()
    return nc


def _build_body(nc):
    S = Sched(nc)

    def din(name, shape):
        return nc.dram_tensor(name, list(shape), F32, kind="ExternalInput").ap()

    def dout(name, shape):
        return nc.dram_tensor(name, list(shape), F32, kind="ExternalOutput").ap()

    xin = din("xin", [TOK, D])
    pin = din("pin", [DEPTH, TOK, 256])
    sconv = din("sconv", [DEPTH, 48, 1536])
    sssm = din("sssm", [DEPTH, 16, 16, 64, 128])
    spool = din("spool", [DEPTH, 240, 512])
    cst = din("cst", [NCST, 128, 128])
    W = {}
    for name, shape in [
        ("norm_mix", [DEPTH, D]), ("w_in", [DEPTH, D, IN_DIM]), ("conv_w", [DEPTH, 4, 1536]),
        ("conv_b", [DEPTH, 1536]), ("dt_bias", [DEPTH, 16]), ("a_log", [DEPTH, 16]), ("d_skip", [DEPTH, 16]),
        ("ssd_norm", [DEPTH, D]), ("sgu_ln_g", [DEPTH, 512]), ("sgu_ln_b", [DEPTH, 512]),
        ("w_spatial", [DEPTH, 4, 128, 128]), ("b_spatial", [DEPTH, 4, 128]), ("pool_w", [DEPTH, 4, 128, 128]),
        ("pool_scale", [DEPTH, 512]), ("w_br_a", [DEPTH, D, D]), ("w_br_b", [DEPTH, 512, D]),
        ("w_br_c", [DEPTH, 512, D]), ("w_out", [DEPTH, D, D]), ("norm_ffn", [DEPTH, D]),
        ("w_gate_up", [DEPTH, D, 2 * D_FF]), ("w_down", [DEPTH, D_FF, D]), ("norm_ple", [DEPTH, D]),
        ("w_ple_gate", [DEPTH, D, D]), ("w_ple_up", [DEPTH, 256, D]), ("final_norm", [D]),
    ]:
        W[name] = din(name, shape)
    yout = dout("yout", [TOK, D])
    convp = dout("convp", [DEPTH, 3, 1536])
    ssmp = dout("ssmp", [DEPTH, 1024, 128])
    poolp = dout("poolp", [DEPTH, 15, 512])
    vp = dout("vp", [DEPTH, 128, 512])
    convs = dout("convs", [DEPTH, 48, 1536])
    ssms = dout("ssms", [DEPTH, 16, 16, 64, 128])
    pools = dout("pools", [DEPTH, 16, 15, 512])
    vs = dout("vs", [DEPTH, 128, 512])

    def sb(name, shape, dt=F32):
        return nc.alloc_sbuf_tensor(name, list(shape), dt), Buf(name)

    xT, b_x = sb("xT", [128, 8, TOK])
    c32, b_c32 = sb("c32", [128, 8, 128])
    cbf, b_cbf = sb("cbf", [128, NCST, 128], BF16)
    pcol, b_pcol = sb("pcol", [128, DEPTH, 96])
    fcol, b_fcol = sb("fcol", [128, 8])
    ptile, b_ptile = sb("ptile", [96, 128])
    bcp, b_bcp = sb("bcp", [128, 1584])
    NSLOT = 4
    SLOTN = 3072
    slots = [sb(f"wslot{i}", [128, SLOTN], BF16) for i in range(NSLOT)]
    hT, b_hT = sb("hT", [128, 8, 512], BF16)
    yaT, b_yaT = sb("yaT", [128, 8, 512], BF16)
    halo, b_halo = sb("halo", [128, 12, 3])
    xcprev, b_xcprev = sb("xcprev", [128, 512], BF16)
    prevT32, b_prev32 = sb("prevT32", [128, 1024])
    prevTb, b_prevb = sb("prevTb", [128, 1024], BF16)
    big = [sb(f"big{i}", [128, 2048]) for i in range(1)]
    pstage = [sb(f"pstage{i}", [128, 256]) for i in range(1)]
    ostage = [sb(f"ostage{i}", [128, 512]) for i in range(2)]
    cstage, b_cstage = big[0][0][0:48, 0:768], big[0][1]
    b_d2d = Buf("d2d")
    spAb, b_spAb = sb("spAb", [120, 512], BF16)
    spBb, b_spBb = sb("spBb", [120, 512], BF16)
    w8, b_w8 = sb("w8", [128, 4, 8])
    wsp, b_wsp = sb("wsp", [128, 4, 128])
    wmT, b_wmT = sb("wmT", [128, 4, 128], BF16)
    wmTs, b_wmTs = sb("wmTs", [128, 4, 128], BF16)
    REGN = 12800
    reg, _ = sb("reg", [128, REGN])

    ps = []
    for i in range(8):
        ps.append((nc.alloc_psum_tensor(f"ps{i}", [128, 512], F32), Buf(f"ps{i}", excl=True)))
    rr = {"mm": 0, "aux": 0, "slot": 0, "big": 0, "ost": 0, "pst": 0}

    def psum(pool="aux"):
        if pool == "mm":
            i = rr["mm"] % 4
            rr["mm"] += 1
            return ps[i]
        if pool == "long0":
            return ps[4]
        if pool == "long1":
            return ps[5]
        i = 6 + rr["aux"] % 2
        rr["aux"] += 1
        return ps[i]

    def nextof(lst, key):
        i = rr[key] % len(lst)
        rr[key] += 1
        return lst[i]

    class Region:
        def __init__(self, off=0, lim=None):
            self.off = off
            self.lim = REGN if lim is None else lim

        def sub(self, off, size):
            return Region(off, off + size)

        def reset(self):
            self.off = 0
            S.barrier()

        def f32(self, name, shape):
            n = int(np.prod(shape[1:]))
            assert self.off + n <= self.lim, (name, self.off, n)
            ap = reg[:, self.off:self.off + n]
            self.off += n
            return self._shape(ap, shape), Buf(name)

        def bf(self, name, shape):
            n = int(np.prod(shape[1:]))
            n32 = (n + 1) // 2
            assert self.off + n32 <= self.lim, (name, self.off, n32)
            ap = reg[:, self.off:self.off + n32].bitcast(BF16)[:, 0:n]
            self.off += n32
            return self._shape(ap, shape), Buf(name)

        @staticmethod
        def _shape(ap, shape):
            if len(shape) == 2:
                return ap
            if len(shape) == 3:
                return ap.rearrange("p (a b) -> p a b", b=shape[2])
            if len(shape) == 4:
                return ap.rearrange("p (a b c) -> p a b c", b=shape[2], c=shape[3])
            raise ValueError

    R = Region()
    ybT = reg[:, REGN - 2048:REGN - 1024].bitcast(BF16).rearrange("p (a b) -> p a b", b=512)
    ycT = reg[:, REGN - 1024:REGN].bitcast(BF16).rearrange("p (a b) -> p a b", b=512)
    b_ybT, b_ycT = Buf("ybT"), Buf("ycT")

    def act(out, in_, func, reads, writes, bias=0.0, scale=1.0, accum_out=None):
        kw = {}
        if accum_out is not None:
            kw["accum_out"] = accum_out
        return S.op("act", lambda e: e.activation(out=out, in_=in_, func=func, bias=bias, scale=scale, **kw),
                    reads=reads, writes=writes)

    def tt(out, in0, in1, op, reads, writes, eng="dve"):
        return S.op(eng, lambda e: e.tensor_tensor(out=out, in0=in0, in1=in1, op=op), reads=reads, writes=writes)

    def ts(out, in0, s1, s2, op0, op1, reads, writes, eng="dve", accum_out=None):
        kw = {}
        if accum_out is not None:
            kw["accum_out"] = accum_out
        if op1 is None:
            return S.op(eng, lambda e: e.tensor_scalar(out=out, in0=in0, scalar1=s1, scalar2=None, op0=op0, **kw),
                        reads=reads, writes=writes)
        return S.op(eng, lambda e: e.tensor_scalar(out=out, in0=in0, scalar1=s1, scalar2=s2, op0=op0, op1=op1, **kw),
                    reads=reads, writes=writes)

    def stt(out, in0, scalar, in1, op0, op1, reads, writes, eng="dve"):
        return S.op(eng, lambda e: e.scalar_tensor_tensor(out=out, in0=in0, scalar=scalar, in1=in1, op0=op0, op1=op1),
                    reads=reads, writes=writes)

    def cp(out, in_, reads, writes, eng="dve"):
        if eng == "act":
            return act(out, in_, AF.Copy, reads, writes)
        return S.op(eng, lambda e: e.tensor_copy(out=out, in_=in_), reads=reads, writes=writes)

    def memset(ap, val, writes, eng="dve"):
        return S.op(eng, lambda e: e.memset(ap, val), writes=writes)

    def mm(out, pairs, reads, writes):
        def fn(e):
            r = None
            n = len(pairs)
            for i, (l, rh) in enumerate(pairs):
                r = e.matmul(out, lhsT=l, rhs=rh, start=(i == 0), stop=(i == n - 1))
            return r
        return S.op("pe", fn, reads=reads, writes=writes)

    def transposes(items, reads, writes):
        def fn(e):
            r = None
            for o, i, idn in items:
                r = e.transpose(o, i, idn)
            return r
        return S.op("pe", fn, reads=reads, writes=writes)

    ident32 = c32[:, C_ID, :]
    identb = cbf[:, C_ID, :]
    onesb = cbf[:, C_ONES, :]

    def bcast_mid(ap2d, n):
        return ap2d.unsqueeze(1).to_broadcast([128, n, ap2d.shape[1]])

    def perhead(eng, out2d, in2d, vec, op, reads, writes, in1_2d=None, op1=None):
        def v3(a):
            return a[:, 0:512].rearrange("p (h d) -> p h d", d=64)
        vb = vec.unsqueeze(2).to_broadcast([128, 8, 64])
        if not FAST_PERHEAD:
            def fn(e):
                r = None
                for h in range(8):
                    sl = slice(h * 64, (h + 1) * 64)
                    if eng == "act":
                        r = e.activation(out=out2d[:, sl], in_=in2d[:, sl], func=AF.Copy, scale=vec[:, h:h + 1])
                    elif in1_2d is None:
                        r = e.tensor_scalar(out=out2d[:, sl], in0=in2d[:, sl], scalar1=vec[:, h:h + 1], scalar2=None, op0=op)
                    else:
                        r = e.scalar_tensor_tensor(out=out2d[:, sl], in0=in2d[:, sl], scalar=vec[:, h:h + 1],
                                                   in1=in1_2d[:, sl], op0=op, op1=op1)
                return r
            return S.op(eng, fn, reads=reads, writes=writes)

        def fn(e):
            r = e.tensor_tensor(out=v3(out2d), in0=v3(in2d), in1=vb, op=op)
            if in1_2d is not None:
                r = e.tensor_tensor(out=v3(out2d), in0=v3(out2d), in1=v3(in1_2d), op=op1)
            return r
        return S.op("dve", fn, reads=reads, writes=writes)

    def hdv(ap2d, d=64):
        return ap2d.rearrange("p (h d) -> p d h", d=d)

    def bc_hd(vec, d=64):
        return vec.unsqueeze(1).to_broadcast([128, d, vec.shape[1]])

    def wload(parts):
        t, b = nextof(slots, "slot")
        K = parts[0][0].shape[0]
        KC = K // 128
        tot = sum(p[2] for p in parts)
        assert KC * tot <= SLOTN
        view = t[:, 0:KC * tot].rearrange("p (c f) -> p c f", f=tot)
        o = 0
        for i, (w2d, c0, ncol) in enumerate(parts):
            src = w2d.rearrange("(c p) f -> p c f", p=128)[:, :, c0:c0 + ncol]
            S.dma("pool", view[:, :, o:o + ncol], src, b, True, cont=(i > 0))
            o += ncol
        return view, b

    def wload2(parts):
        t, b = nextof(slots, "slot")
        o = 0
        views = []
        for i, (w2d, c0, ncol) in enumerate(parts):
            KC = w2d.shape[0] // 128
            n = KC * ncol
            assert o + n <= SLOTN
            view = t[:, o:o + n].rearrange("p (c f) -> p c f", f=ncol)
            S.dma("pool", view, w2d.rearrange("(c p) f -> p c f", p=128)[:, :, c0:c0 + ncol], b, True, cont=(i > 0))
            views.append(view)
            o += n
        return views, b

    cview = cst.rearrange("c p f -> p c f")
    S.dma("sp", c32[:], cview[:, 0:8, :], b_c32, True)
    S.dma("pool", cbf[:], cview, b_cbf, True)
    memset(halo[:], 0.0, [b_halo])

    PC_NM, PC_NF, PC_NP, PC_SN, PC_CW, PC_CB, PC_PS = 0, 8, 16, 24, 32, 80, 92
    for L in range(RUN_DEPTH):
        for (nm, r0, nr) in (("norm_mix", PC_NM, 8), ("norm_ffn", PC_NF, 8), ("norm_ple", PC_NP, 8),
                             ("ssd_norm", PC_SN, 8), ("conv_b", PC_CB, 12), ("pool_scale", PC_PS, 4)):
            S.dma("sp", ptile[r0:r0 + nr, :], W[nm][L].rearrange("(c p) -> c p", p=128), b_ptile, True)
        S.dma("sp", ptile[PC_CW:PC_CW + 48, :], W["conv_w"][L].rearrange("k (c p) -> (k c) p", p=128), b_ptile, True)
        pt, bpt = psum()
        transposes([(pt[:, 0:96], ptile[:, :], ident32[0:96, 0:96])], [b_ptile, b_c32], [bpt])
        cp(pcol[:, L, :], pt[:, 0:96], [bpt], [b_pcol])
    S.dma("sp", ptile[0:8, :], W["final_norm"].rearrange("(c p) -> c p", p=128), b_ptile, True)
    pt, bpt = psum()
    transposes([(pt[:, 0:8], ptile[0:8, :], ident32[0:8, 0:8])], [b_ptile, b_c32], [bpt])
    cp(fcol[:], pt[:, 0:8], [bpt], [b_fcol])

    for t in range(NTILE):
        st, bst = nextof(big, "big")
        S.dma("sp", st[:, 0:1024], xin[t * 128:(t + 1) * 128, :], bst, True)
        for hb in range(2):
            p_, bp_ = psum()
            transposes([(p_[:, j * 128:(j + 1) * 128], st[:, (hb * 4 + j) * 128:(hb * 4 + j + 1) * 128], ident32)
                        for j in range(4)], [bst, b_c32], [bp_])
            cp(xT[:, hb * 4:hb * 4 + 4, t * 128:(t + 1) * 128], p_[:].rearrange("p (a b) -> p a b", b=128),
               [bp_], [b_x], eng=("act" if hb else "dve"))

    def rmsnorm_T(tok0, NT, gcol_of, dst, b_dst, out_is_f32=False):
        sq = [R.bf(f"sq{i}", [128, NT]) for i in range(2)]
        ms, b_ms = R.f32("ms", [128, NT])
        pss, bpss = psum("long0")
        for c in range(8):
            q, bq = sq[c % 2]
            act(q, xT[:, c, tok0:tok0 + NT], AF.Square, [b_x], [bq])
            S.op("pe", lambda e, q=q, c=c: e.matmul(pss[:, 0:NT], lhsT=onesb, rhs=q, start=(c == 0), stop=(c == 7)),
                 reads=[bq, b_cbf], writes=[bpss])
        ts(ms, pss[:, 0:NT], 1.0 / D, EPS, ALU.mult, ALU.add, [bpss], [b_ms])
        act(ms, ms, AF.Sqrt, [b_ms], [b_ms])
        S.op("dve", lambda e: e.reciprocal(out=ms, in_=ms), reads=[b_ms], writes=[b_ms])
        for c in range(8):
            stt(dst[:, c, 0:NT], xT[:, c, tok0:tok0 + NT], gcol_of(c), ms, ALU.mult, ALU.mult,
                [b_x, b_ms, b_pcol, b_fcol], [b_dst])

    def linear_fm(wparts, actT, b_act, KC, NT, nchunks, consume, pool="mm"):
        view, bw = wload(wparts)
        for j in range(nchunks):
            p_, bp_ = psum(pool)
            mm(p_[:, 0:NT], [(view[:, kc, j * 128:(j + 1) * 128], actT[:, kc, 0:NT]) for kc in range(KC)],
               [bw, b_act], [bp_])
            consume(j, p_, bp_)

    for L in range(RUN_DEPTH):
        w_in = W["w_in"][L]
        pc = lambda col: pcol[:, L, col:col + 1]
        for (nm, o, n) in (("dt_bias", 0, 16), ("a_log", 16, 16), ("d_skip", 32, 16), ("sgu_ln_g", 48, 512),
                           ("sgu_ln_b", 560, 512)):
            S.dma("sp", bcp[:, o:o + n], W[nm][L:L + 1, :].to_broadcast([128, n]), b_bcp, True)
        S.dma("sp", bcp[:, 1072:1584], W["b_spatial"][L:L + 1].rearrange("o g t -> o (g t)").to_broadcast([128, 512]),
              b_bcp, True)
        act(bcp[:, 16:32], bcp[:, 16:32], AF.Exp, [b_bcp], [b_bcp])
        ts(bcp[:, 16:32], bcp[:, 16:32], -1.0, None, ALU.mult, None, [b_bcp], [b_bcp])
        dtb_bc, a_bc, dsk_bc = bcp[:, 0:16], bcp[:, 16:32], bcp[:, 32:48]
        lng_bc, lnb_bc, bsp_bc = bcp[:, 48:560], bcp[:, 560:1072], bcp[:, 1072:1584]
        S.dma("sp", wsp[:], W["w_spatial"][L].rearrange("g t s -> t g s"), b_wsp, True)
        pw, bpw = psum()
        transposes([(pw[:, g * 128:(g + 1) * 128], wsp[:, g, :], ident32) for g in range(4)], [b_wsp, b_c32], [bpw])
        tt(wmT[:], pw[:].rearrange("p (g t) -> p g t", t=128), bcast_mid(c32[:, C_U, :], 4), ALU.mult,
           [bpw, b_c32], [b_wmT])
        for j in range(16):
            S.dma("sp", w8[j * 8:(j + 1) * 8, :, :], W["w_spatial"][L][:, 0:8, 0:8].rearrange("g t s -> t g s"),
                  b_w8, True)
        for g in range(4):
            cp(wsp[:, g, :].rearrange("p (j s) -> p j s", s=8), bcast_mid(w8[:, g, :], 16), [b_w8], [b_wsp])
        pw, bpw = psum()
        transposes([(pw[:, g * 128:(g + 1) * 128], wsp[:, g, :], ident32) for g in range(4)], [b_wsp, b_c32], [bpw])
        tt(wmTs[:], pw[:].rearrange("p (g t) -> p g t", t=128), bcast_mid(c32[:, C_US, :], 4), ALU.mult,
           [bpw, b_c32], [b_wmTs])
        memset(prevT32[:], 0.0, [b_prev32])
        memset(prevTb[:], 0.0, [b_prevb])
        memset(halo[:], 0.0, [b_halo])
        memset(xcprev[:], 0.0, [b_xcprev])

        for (tile0, nt, is_s) in GROUPS:
            NT = nt * 128
            tok0 = tile0 * 128
            last_prompt = (tile0 + nt == 16)
            R.reset()
            rmsnorm_T(tok0, NT, lambda c: pc(PC_NM + c), hT, b_hT)
            for sg in (range(2) if STAGES >= 2 else []):
                R.reset()
                zs, b_zs = R.f32("zs", [128, nt, 512])
                xc_off = R.off
                if is_s:
                    xcin, b_xcin = R.f32("xcin", [128, 6, 16, 11])
                else:
                    xcin, b_xcin = R.f32("xcin", [128, 6, NT + 3])
                xc_size = R.off - xc_off
                xsT, b_xsT = R.f32("xsT", [128, 4, NT])
                BT, b_BT = R.bf("BT", [128, NT])
                CT, b_CT = R.bf("CT", [128, NT])
                dtr, b_dtr = R.f32("dtr", [128, nt, 8])
                dtt, b_dtt = R.f32("dtt", [128, nt, 8])
                daa, b_daa = R.f32("daa", [128, nt, 8])
                tmp8, b_tmp8 = R.f32("tmp8", [128, nt, 8])
                cacc, b_cacc = R.f32("cacc", [128, NT])
                if is_s:
                    ctmp, b_ctmp = R.f32("ctmp", [128, 6, 48])
                vz1, bwz1 = wload([(w_in, O_Z + sg * 512, 384)])
                vz2, bwz2 = wload([(w_in, O_Z + sg * 512 + 384, 128)])
                for t in range(nt):
                    p_, bp_ = psum("mm")
                    mm(p_[:, 0:384], [(hT[:, kc, t * 128:(t + 1) * 128], vz1[:, kc, :]) for kc in range(8)],
                       [bwz1, b_hT], [bp_])
                    mm(p_[:, 384:512], [(hT[:, kc, t * 128:(t + 1) * 128], vz2[:, kc, :]) for kc in range(8)],
                       [bwz2, b_hT], [bp_])
                    act(zs[:, t, :], p_[:, :], AF.Silu, [bp_], [b_zs])

                def xc_dst(j):
                    if is_s:
                        return xcin[:, j, :, 3:11]
                    return xcin[:, j, 3:3 + NT]

                def ps_src(p_):
                    if is_s:
                        return p_[:, 0:NT].rearrange("p (j t) -> p j t", t=8)
                    return p_[:, 0:NT]

                vx1, bwx1 = wload([(w_in, O_XBC + sg * 512, 384)])
                vx2, bwx2 = wload([(w_in, O_XBC + sg * 512 + 384, 128)])
                for j in range(4):
                    p_, bp_ = psum("mm")
                    vv_, bb_, jj_ = (vx1, bwx1, j) if j < 3 else (vx2, bwx2, 0)
                    mm(p_[:, 0:NT], [(vv_[:, kc, jj_ * 128:(jj_ + 1) * 128], hT[:, kc, 0:NT]) for kc in range(8)],
                       [bb_, b_hT], [bp_])
                    cp(xc_dst(j), ps_src(p_), [bp_], [b_xcin], eng=("act" if j % 2 else "dve"))
                vbc, bwbc = wload([(w_in, O_XBC + 1024 + sg * 128, 128), (w_in, O_XBC + 1280 + sg * 128, 128),
                                   (w_in, O_DT + sg * 8, 8)])
                for j in range(4, 6):
                    p_, bp_ = psum("mm")
                    o = (j - 4) * 128
                    mm(p_[:, 0:NT], [(vbc[:, kc, o:o + 128], hT[:, kc, 0:NT]) for kc in range(8)],
                       [bwbc, b_hT], [bp_])
                    cp(xc_dst(j), ps_src(p_), [bp_], [b_xcin], eng=("act" if j % 2 else "dve"))
                for t in range(nt):
                    p_, bp_ = psum("mm")
                    mm(p_[:, 0:8], [(hT[:, kc, t * 128:(t + 1) * 128], vbc[:, kc, 256:264]) for kc in range(8)],
                       [bwbc, b_hT], [bp_])
                    cp(dtr[:, t, :], p_[:, 0:8], [bp_], [b_dtr])
                if SSDLVL < 2:
                    continue
                chunk_ids = [sg * 4 + j for j in range(4)] + [8 + sg, 10 + sg]
                if is_s:
                    S.dma("sp", cstage[:, 0:512], sconv[L][:, sg * 512:(sg + 1) * 512], b_cstage, True)
                    S.dma("sp", cstage[:, 512:640], sconv[L][:, 1024 + sg * 128:1024 + (sg + 1) * 128], b_cstage, True)
                    S.dma("sp", cstage[:, 640:768], sconv[L][:, 1280 + sg * 128:1280 + (sg + 1) * 128], b_cstage, True)
                    for jj, cid in enumerate(chunk_ids):
                        p_, bp_ = psum()
                        transposes([(p_[:, 0:48], cstage[:, jj * 128:(jj + 1) * 128], ident32[0:48, 0:48])],
                                   [b_cstage, b_c32], [bp_])
                        cp(xcin[:, jj, :, 0:3], p_[:, 0:48].rearrange("p (j r) -> p j r", r=3), [bp_], [b_xcin])
                else:
                    for jj, cid in enumerate(chunk_ids):
                        cp(xcin[:, jj, 0:3], halo[:, cid, :], [b_halo], [b_xcin], eng="act")
                if last_prompt or is_s:
                    ncol = 48 if is_s else 3
                    for half in range(2):
                        ost, bost = nextof(ostage, "ost")
                        p_, bp_ = psum()
                        items = []
                        for q in range(3):
                            jj = half * 3 + q
                            if is_s:
                                cp(ctmp[:, jj, :].rearrange("p (j r) -> p j r", r=3), xcin[:, jj, :, 8:11],
                                   [b_xcin], [b_ctmp])
                                src = ctmp[:, jj, :]
                            else:
                                src = xcin[:, jj, NT:NT + 3]
                            items.append((p_[0:ncol, q * 128:(q + 1) * 128], src, ident32))
                        transposes(items, [b_xcin, b_c32] + ([b_ctmp] if is_s else []), [bp_])
                        cp(ost[0:ncol, 0:384], p_[0:ncol, 0:384], [bp_], [bost])
                        for q in range(3):
                            cid = chunk_ids[half * 3 + q]
                            dst = (convs if is_s else convp)[L][:, cid * 128:(cid + 1) * 128]
                            S.dma("sp", dst, ost[0:ncol, q * 128:(q + 1) * 128], bost, False)
                if not is_s:
                    for jj, cid in enumerate(chunk_ids):
                        cp(halo[:, cid, :], xcin[:, jj, NT:NT + 3], [b_xcin], [b_halo], eng="act")
                if SSDLVL < 3:
                    continue
                for jj, cid in enumerate(chunk_ids):
                    def win(k):
                        if is_s:
                            return xcin[:, jj, :, k:k + 8]
                        return xcin[:, jj, k:k + NT]
                    cacc_v = cacc.rearrange("p (j t) -> p j t", t=8) if is_s else cacc
                    ts(cacc_v, win(0), pc(PC_CW + 0 * 12 + cid), None, ALU.mult, None, [b_xcin, b_pcol], [b_cacc])
                    for k in range(1, 4):
                        stt(cacc_v, win(k), pc(PC_CW + k * 12 + cid), cacc_v, ALU.mult, ALU.add,
                            [b_xcin, b_pcol, b_cacc], [b_cacc])
                    if jj < 4:
                        dst, bd = xsT[:, jj, :], b_xsT
                    elif jj == 4:
                        dst, bd = BT, b_BT
                    else:
                        dst, bd = CT, b_CT
                    act(dst, cacc, AF.Silu, [b_cacc, b_pcol], [bd], bias=pc(PC_CB + cid))
                if SSDLVL < 4:
                    continue
                dtb = bcast_mid(dtb_bc[:, sg * 8:sg * 8 + 8], nt)
                tt(dtr, dtr, dtb, ALU.add, [b_dtr, b_bcp], [b_dtr])
                act(tmp8, dtr, AF.Abs, [b_dtr], [b_tmp8])
                act(tmp8, tmp8, AF.Exp, [b_tmp8], [b_tmp8], scale=-1.0)
                act(tmp8, tmp8, AF.Ln, [b_tmp8], [b_tmp8], bias=1.0)
                stt(dtt, dtr, 0.0, tmp8, ALU.max, ALU.add, [b_dtr, b_tmp8], [b_dtt])
                tt(daa, dtt, bcast_mid(a_bc[:, sg * 8:sg * 8 + 8], nt), ALU.mult, [b_dtt, b_bcp], [b_daa])
                if SSDLVL < 5:
                    continue
                Ucs = c32[:, C_US, :] if is_s else c32[:, C_U, :]
                Otot = c32[:, C_OS, :] if is_s else c32[:, C_ONES, :]
                cs16, b_cs16 = R.f32("cs16", [128, 16])
                ecs, b_ecs = R.f32("ecs", [128, 8])
                dec, b_dec = R.f32("dec", [128, 8])
                edec, b_edec = R.f32("edec", [128, 8])
                dtdec, b_dtdec = R.f32("dtdec", [128, 8])
                xs32, b_xs32 = R.f32("xs32", [128, 512])
                Xd, b_Xd = R.bf("Xd", [128, 512])
                Xdd, b_Xdd = R.bf("Xdd", [128, 512])
                Btok, b_Btok = R.bf("Btok", [128, 128])
                CBm, b_CBm = R.f32("CBm", [128, 128])
                R1, b_R1 = R.bf("R1", [128, 8, 128])
                R2, b_R2 = R.bf("R2", [128, 8, 128])
                Lt, b_Lt = R.bf("Lt", [128, 8, 128])
                MT, b_MT = R.bf("MT", [128, 8, 128])
                t1, b_t1 = R.f32("t1", [128, 512])
                t2, b_t2 = R.f32("t2", [128, 512])
                ss2, b_ss2 = R.f32("ss2", [128, 2])
                if is_s:
                    H0T, b_H0T = R.bf("H0T", [128, 16, 128])
                    CTpad, b_CTpad = R.bf("CTpad", [128, 16, 128])
                    Bmask, b_Bmask = R.bf("Bmask", [128, 16, 128])
                    daexp, b_daexp = R.f32("daexp", [128, 2, 64])
                    Sc, b_Sc = R.f32("Sc", [128, 16])
                    memset(CTpad[:], 0.0, [b_CTpad])
                hand = [dict(ecs=(ecs, b_ecs), edec=(edec, b_edec), xs32=(xs32, b_xs32), Xd=(Xd, b_Xd),
                             Xdd=(Xdd, b_Xdd), Btok=(Btok, b_Btok), MT=(MT, b_MT))]
                if nt > 1:
                    S.barrier()
                    Rb = R.sub(xc_off, xc_size)
                    hand.append(dict(ecs=Rb.f32("ecsb", [128, 8]), edec=Rb.f32("edecb", [128, 8]),
                                     xs32=Rb.f32("xs32b", [128, 512]), Xd=Rb.bf("Xdb", [128, 512]),
                                     Xdd=Rb.bf("Xddb", [128, 512]), Btok=Rb.bf("Btokb", [128, 128]),
                                     MT=Rb.bf("MTb", [128, 8, 128])))

                batched = nt > 1
                if batched:
                    csA, b_csA = Rb.f32("csA", [128, nt, 16])
                    ecsA, b_ecsA = Rb.f32("ecsA", [128, nt, 8])
                    decA, b_decA = Rb.f32("decA", [128, nt, 8])
                    edecA, b_edecA = Rb.f32("edecA", [128, nt, 8])
                    dtdecA, b_dtdecA = Rb.f32("dtdecA", [128, nt, 8])
                    pcA, bpcA = psum()

                    def csfn(e, pcA=pcA, daa=daa, Ucs=Ucs, Otot=Otot, nt=nt):
                        r = None
                        for t in range(nt):
                            e.matmul(pcA[:, t * 16:t * 16 + 8], lhsT=Ucs, rhs=daa[:, t, :], start=True, stop=True)
                            r = e.matmul(pcA[:, t * 16 + 8:t * 16 + 16], lhsT=Otot, rhs=daa[:, t, :], start=True, stop=True)
                        return r
                    S.op("pe", csfn, reads=[b_c32, b_daa], writes=[bpcA])
                    cp(csA, pcA[:, 0:nt * 16].rearrange("p (t c) -> p t c", c=16), [bpcA], [b_csA], eng="act")
                    act(ecsA, csA[:, :, 0:8], AF.Exp, [b_csA], [b_ecsA])
                    tt(decA, csA[:, :, 8:16], csA[:, :, 0:8], ALU.subtract, [b_csA], [b_decA])
                    act(decA, decA, AF.Exp, [b_decA], [b_decA])
                    act(edecA, csA[:, :, 8:16], AF.Exp, [b_csA], [b_edecA])
                    tt(dtdecA, dtt, decA, ALU.mult, [b_dtt, b_decA], [b_dtdecA])

                def tile_stats(t, H):
                    if batched:
                        return (ecsA[:, t, :], b_ecsA, edecA[:, t, :], b_edecA, dtdecA[:, t, :], b_dtdecA)
                    e_, be_ = H["ecs"]
                    d_, bd_ = H["edec"]
                    return (e_, be_, d_, bd_, dtdec, b_dtdec)

                def stageA(t):
                    H = hand[t % len(hand)]
                    ecs_, b_ecs_, edec_, b_edec_, dtdec_, b_dtdec_ = tile_stats(t, H)
                    xs32_, b_xs32_ = H["xs32"]
                    Xd_, b_Xd_ = H["Xd"]
                    Xdd_, b_Xdd_ = H["Xdd"]
                    Btok_, b_Btok_ = H["Btok"]
                    MT_, b_MT_ = H["MT"]
                    tc0 = t * 128
                    da_t = daa[:, t, :]
                    if not batched:
                        pc_, bpc_ = psum()
                        mm(pc_[:, 0:8], [(Ucs, da_t)], [b_c32, b_daa], [bpc_])
                        mm(pc_[:, 8:16], [(Otot, da_t)], [b_c32, b_daa], [bpc_])
                        yield
                        cp(cs16, pc_[:, 0:16], [bpc_], [b_cs16], eng="act")
                        yield
                        act(ecs_, cs16[:, 0:8], AF.Exp, [b_cs16], [b_ecs_])
                        yield
                        tt(dec, cs16[:, 8:16], cs16[:, 0:8], ALU.subtract, [b_cs16], [b_dec])
                        yield
                        act(dec, dec, AF.Exp, [b_dec], [b_dec])
                        yield
                        act(edec_, cs16[:, 8:16], AF.Exp, [b_cs16], [b_edec_])
                        yield
                        tt(dtdec, dtt[:, t, :], dec, ALU.mult, [b_dtt, b_dec], [b_dtdec])
                        yield
                    px, bpx = psum("mm")
                    transposes([(px[:, j * 128:(j + 1) * 128], xsT[:, j, tc0:tc0 + 128], ident32) for j in range(4)],
                               [b_xsT, b_c32], [bpx])
                    yield
                    cp(xs32_, px[:, :], [bpx], [b_xs32_], eng="act")
                    yield
                    perhead("dve", Xd_, px, dtt[:, t, :], ALU.mult, [bpx, b_dtt], [b_Xd_])
                    yield
                    perhead("act", Xdd_, px, dtdec_, ALU.mult, [bpx, b_dtdec_], [b_Xdd_])
                    yield
                    pb_, bpb_ = psum()
                    pbb = pb_[:, 0:64].bitcast(BF16)
                    transposes([(pbb, BT[:, tc0:tc0 + 128], identb)], [b_BT, b_cbf], [bpb_])
                    yield
                    cp(Btok_, pbb, [bpb_], [b_Btok_], eng="act")
                    yield
                    pcb, bpcb = psum()
                    mm(pcb[:, 0:128], [(BT[:, tc0:tc0 + 128], CT[:, tc0:tc0 + 128])], [b_BT, b_CT], [bpcb])
                    yield
                    tt(CBm, pcb[:, 0:128], Ucs, ALU.mult, [bpcb, b_c32], [b_CBm])
                    yield

                    tt(R1, bcast_mid(c32[:, C_U, :], 8), da_t.unsqueeze(2).to_broadcast([128, 8, 128]), ALU.mult,
                       [b_c32, b_daa], [b_R1])
                    yield

                    def r2fn(e, da_t=da_t, R2=R2):
                        r = None
                        for h in range(8):
                            r = e.activation(out=R2[:, h, :], in_=c32[:, C_CM, :], func=AF.Identity,
                                             bias=da_t[:, h:h + 1], scale=1.0)
                        return r
                    S.op("act", r2fn, reads=[b_c32, b_daa], writes=[b_R2])
                    yield
                    for hb in range(2):
                        pl, bpl = psum("mm")
                        mm(pl[:, :], [(onesb, R1.rearrange("p h l -> p (h l)")[:, hb * 512:(hb + 1) * 512]),
                                      (cbf[:, C_NEGU, :], R2.rearrange("p h l -> p (h l)")[:, hb * 512:(hb + 1) * 512])],
                           [b_cbf, b_R1, b_R2], [bpl])
                        yield
                        act(Lt[:, hb * 4:hb * 4 + 4, :], pl[:, :].rearrange("p (h l) -> p h l", l=128), AF.Exp,
                            [bpl], [b_Lt])
                        yield
                    tt(MT_, Lt, bcast_mid(CBm, 8), ALU.mult, [b_Lt, b_CBm], [b_MT_])
                    yield

                def stageB(t):
                    H = hand[t % len(hand)]
                    ecs_, b_ecs_, edec_, b_edec_, dtdec_, b_dtdec_ = tile_stats(t, H)
                    xs32_, b_xs32_ = H["xs32"]
                    Xd_, b_Xd_ = H["Xd"]
                    Xdd_, b_Xdd_ = H["Xdd"]
                    Btok_, b_Btok_ = H["Btok"]
                    MT_, b_MT_ = H["MT"]
                    tc0 = t * 128
                    da_t = daa[:, t, :]
                    py, bpy = psum("long0")

                    def ydiag(e, py=py, MT_=MT_, Xd_=Xd_):
                        r = None
                        for h in range(8):
                            r = e.matmul(py[:, h * 64:(h + 1) * 64], lhsT=MT_[:, h, :], rhs=Xd_[:, h * 64:(h + 1) * 64],
                                         start=True, stop=True)
                        return r
                    S.op("pe", ydiag, reads=[b_MT_, b_Xd_], writes=[bpy])
                    yield
                    pyo, bpyo = psum("long1")
                    if not is_s:
                        mm(pyo[:, :], [(CT[:, tc0:tc0 + 128], prevTb[:, sg * 512:(sg + 1) * 512])],
                           [b_CT, b_prevb], [bpyo])
                        yield
                        pst, bpst = psum("mm")
                        mm(pst[:, :], [(Btok_, Xdd_)], [b_Btok_, b_Xdd_], [bpst])
                        yield
                        pv = prevT32[:, sg * 512:(sg + 1) * 512]
                        perhead("dve", pv, pv, edec_, ALU.mult, [b_prev32, b_edec_, bpst], [b_prev32],
                                in1_2d=pst, op1=ALU.add)
                        yield
                        cp(prevTb[:, sg * 512:(sg + 1) * 512], pv, [b_prev32], [b_prevb], eng="act")
                        yield
                    else:
                        def padfn(e, CTpad=CTpad, CT=CT):
                            r = None
                            for j in range(16):
                                r = e.tensor_copy(out=CTpad[:, j, j * 8:(j + 1) * 8], in_=CT[:, j * 8:(j + 1) * 8])
                            return r
                        S.op("dve", padfn, reads=[b_CT], writes=[b_CTpad])

                        def bmfn(e, Bmask=Bmask, Btok_=Btok_):
                            r = None
                            for j in range(16):
                                r = e.tensor_scalar(out=Bmask[:, j, :], in0=Btok_, scalar1=c32[:, C_IND, j:j + 1],
                                                    scalar2=None, op0=ALU.mult)
                            return r
                        S.op("dve", bmfn, reads=[b_Btok_, b_c32], writes=[b_Bmask])
                        for q in range(4):
                            hp = sg * 4 + q
                            h0, bh0 = nextof(big, "big")
                            h0v = h0[:, :].rearrange("p (j n) -> p j n", n=128)
                            S.dma("sp", h0v, sssm[L][:, 2 * hp:2 * hp + 2].rearrange("j h p n -> (h p) j n"), bh0, True)
                            for qq in range(4):
                                ph, bph = psum()
                                transposes([(ph[:, i * 128:(i + 1) * 128], h0v[:, qq * 4 + i, :], ident32)
                                            for i in range(4)], [bh0, b_c32], [bph])
                                cp(H0T[:, qq * 4:qq * 4 + 4, :], ph[:, :].rearrange("p (j n) -> p j n", n=128),
                                   [bph], [b_H0T], eng=("act" if qq % 2 else "dve"))
                            mm(pyo[:, q * 128:(q + 1) * 128], [(CTpad[:, j, :], H0T[:, j, :]) for j in range(16)],
                               [b_CTpad, b_H0T], [bpyo])

                            def dxfn(e, q=q, da_t=da_t, daexp=daexp):
                                r = None
                                for h2 in range(2):
                                    r = e.activation(out=daexp[:, h2, :], in_=c32[:, C_ONES, 0:64], func=AF.Copy,
                                                     scale=da_t[:, 2 * q + h2:2 * q + h2 + 1])
                                return r
                            S.op("act", dxfn, reads=[b_daa, b_c32], writes=[b_daexp])
                            psc, bpsc = psum()
                            mm(psc[:, 0:16], [(daexp[:, :, :].rearrange("p h d -> p (h d)"), c32[:, C_IND, 0:16])],
                               [b_daexp, b_c32], [bpsc])
                            act(Sc, psc[:, 0:16], AF.Exp, [bpsc], [b_Sc])
                            for half in range(4):
                                pst, bpst = psum()
                                mm(pst[:, :], [(Xdd_[:, q * 128:(q + 1) * 128],
                                                Bmask.rearrange("p j n -> p (j n)")[:, half * 512:(half + 1) * 512])],
                                   [b_Xdd_, b_Bmask], [bpst])
                                ost, bost = nextof(ostage, "ost")
                                ov = ost[:, :].rearrange("p (j n) -> p j n", n=128)

                                def snfn(e, ov=ov, h0v=h0v, half=half, pst=pst, Sc=Sc):
                                    r = None
                                    for jj in range(4):
                                        j = half * 4 + jj
                                        r = e.scalar_tensor_tensor(out=ov[:, jj, :], in0=h0v[:, j, :], scalar=Sc[:, j:j + 1],
                                                                   in1=pst[:, jj * 128:(jj + 1) * 128], op0=ALU.mult,
                                                                   op1=ALU.add)
                                    return r
                                S.op("dve", snfn, reads=[bh0, b_Sc, bpst], writes=[bost])
                                S.dma("sp", ssms[L][half * 4:half * 4 + 4, 2 * hp:2 * hp + 2]
                                      .rearrange("j h p n -> (h p) j n"), ov, bost, False)
                        yield
                    perhead("act", t2, pyo, ecs_, ALU.mult, [bpyo, b_ecs_], [b_t2])
                    yield
                    tt(t1, t2, py[:, :], ALU.add, [b_t2, bpy], [b_t1])
                    yield
                    perhead("dve", t2, xs32_, dsk_bc[:, sg * 8:sg * 8 + 8], ALU.mult, [b_xs32_, b_bcp], [b_t2])
                    yield
                    tt(t1, t1, t2, ALU.add, [b_t1, b_t2], [b_t1])
                    yield
                    tt(t1, t1, zs[:, t, :], ALU.mult, [b_t1, b_zs], [b_t1])
                    yield
                    act(t2, t1, AF.Square, [b_t1], [b_t2])
                    yield
                    S.op("dve", lambda e, ss2=ss2, t2=t2: e.reduce_sum(out=ss2[:, 0:1], in_=t2, axis=mybir.AxisListType.X),
                         reads=[b_t2], writes=[b_ss2])
                    yield
                    ts(ss2[:, 1:2], ss2[:, 0:1], 1.0 / 512, EPS, ALU.mult, ALU.add, [b_ss2], [b_ss2])
                    yield
                    act(ss2[:, 1:2], ss2[:, 1:2], AF.Sqrt, [b_ss2], [b_ss2])
                    yield
                    S.op("dve", lambda e, ss2=ss2: e.reciprocal(out=ss2[:, 1:2], in_=ss2[:, 1:2]), reads=[b_ss2], writes=[b_ss2])
                    yield
                    ts(t1, t1, ss2[:, 1:2], None, ALU.mult, None, [b_t1, b_ss2], [b_t1])
                    yield
                    pyt, bpyt = psum("mm")
                    transposes([(pyt[:, j * 128:(j + 1) * 128], t1[:, j * 128:(j + 1) * 128], ident32) for j in range(4)],
                               [b_t1, b_c32], [bpyt])
                    yield

                    tt(yaT[:, sg * 4:sg * 4 + 4, tc0:tc0 + 128], pyt[:, :].rearrange("p (c t) -> p c t", t=128),
                       pcol[:, L, PC_SN + sg * 4:PC_SN + sg * 4 + 4].unsqueeze(2).to_broadcast([128, 4, 128]), ALU.mult,
                       [bpyt, b_pcol], [b_yaT])
                    yield

                def drain(g):
                    for _ in g:
                        pass

                if nt == 1 or not PIPELINE:
                    for t in range(nt):
                        drain(stageA(t))
                        drain(stageB(t))
                else:
                    drain(stageA(0))
                    for t in range(nt):
                        gb = stageB(t)
                        ga = stageA(t + 1) if t + 1 < nt else iter(())
                        while True:
                            a_end = next(ga, "END") == "END"
                            b_end = next(gb, "END") == "END"
                            if a_end and b_end:
                                break
                if last_prompt:
                    pf, bpf = psum()
                    transposes([(pf[:, q * 128:(q + 1) * 128], prevT32[:, sg * 512 + q * 128:sg * 512 + (q + 1) * 128],
                                 ident32) for q in range(4)], [b_prev32, b_c32], [bpf])
                    ost, bost = nextof(ostage, "ost")
                    cp(ost[:, :], pf[:, :], [bpf], [bost])
                    S.dma("sp", ssmp[L].rearrange("(q r) n -> r q n", r=128)[:, sg * 4:(sg + 1) * 4, :],
                          ost[:, :].rearrange("p (q n) -> p q n", n=128), bost, False)

            if STAGES < 3:
                continue
            R.reset()
            uT, b_uT = R.f32("uT", [128, 4, NT])
            vnb, b_vnb = R.bf("vnb", [128, nt, 512])
            g1, b_g1 = R.f32("g1", [128, 512])
            g2, b_g2 = R.f32("g2", [128, 512])
            v32, b_v32 = R.f32("v32", [128, 512])
            st2, b_st2 = R.f32("st2", [128, 4])
            xcb, b_xcb = R.bf("xcb", [128, nt, 512])
            dT, b_dT = R.bf("dT", [128, 4, NT])

            def gelu_from(p_ap, bp_, out_ap, b_out, n):
                if FAST_GELU:
                    act(out_ap, p_ap, AF.Gelu_apprx_tanh, [bp_], [b_out])
                    return
                a, c_ = g1[:, 0:n], g2[:, 0:n]
                act(a, p_ap, AF.Square, [bp_], [b_g1])
                ts(a, a, 0.044715, 1.0, ALU.mult, ALU.add, [b_g1], [b_g1])
                tt(a, a, p_ap, ALU.mult, [b_g1, bp_], [b_g1])
                act(c_, a, AF.Sigmoid, [b_g1], [b_g2], scale=2.0 * math.sqrt(2.0 / math.pi))
                tt(out_ap, c_, p_ap, ALU.mult, [b_g2, bp_], [b_out])

            vu, bwu = wload([(w_in, O_UV, 384)])
            vu2, bwu2 = wload([(w_in, O_UV + 384, 128)])
            for j in range(4):
                p_, bp_ = psum("mm")
                vv, bb, jj = (vu, bwu, j) if j < 3 else (vu2, bwu2, 0)
                mm(p_[:, 0:NT], [(vv[:, kc, jj * 128:(jj + 1) * 128], hT[:, kc, 0:NT]) for kc in range(8)],
                   [bb, b_hT], [bp_])
                gelu_from(p_[:, 0:NT], bp_, uT[:, j, :], b_uT, NT)
            vv1, bwv1 = wload([(w_in, O_UV + 512, 384)])
            vv2, bwv2 = wload([(w_in, O_UV + 896, 128)])
            vset = [(v32, b_v32, g1, b_g1, st2, b_st2)]
            if nt > 1:
                v32b, b_v32b = R.f32("v32b", [128, 512])
                g1b, b_g1b = R.f32("g1b", [128, 512])
                st2b, b_st2b = R.f32("st2b", [128, 4])
                vset.append((v32b, b_v32b, g1b, b_g1b, st2b, b_st2b))

            def vstage(t):
                v32_, b_v32_, g1_, b_g1_, st2_, b_st2_ = vset[t % len(vset)]
                p_, bp_ = psum("mm")
                mm(p_[:, 0:384], [(hT[:, kc, t * 128:(t + 1) * 128], vv1[:, kc, :]) for kc in range(8)],
                   [bwv1, b_hT], [bp_])
                mm(p_[:, 384:512], [(hT[:, kc, t * 128:(t + 1) * 128], vv2[:, kc, :]) for kc in range(8)],
                   [bwv2, b_hT], [bp_])
                yield
                if FAST_GELU:
                    act(v32_, p_[:, :], AF.Gelu_apprx_tanh, [bp_], [b_v32_])
                else:
                    gelu_from(p_[:, :], bp_, v32_, b_v32_, 512)
                yield
                S.op("dve", lambda e, st2_=st2_, v32_=v32_: e.reduce_sum(out=st2_[:, 0:1], in_=v32_, axis=mybir.AxisListType.X),
                     reads=[b_v32_], writes=[b_st2_])
                yield
                ts(st2_[:, 1:2], st2_[:, 0:1], -1.0 / 512, None, ALU.mult, None, [b_st2_], [b_st2_])
                yield
                ts(v32_, v32_, st2_[:, 1:2], None, ALU.add, None, [b_v32_, b_st2_], [b_v32_])
                yield
                act(g1_, v32_, AF.Square, [b_v32_], [b_g1_])
                yield
                S.op("dve", lambda e, st2_=st2_, g1_=g1_: e.reduce_sum(out=st2_[:, 2:3], in_=g1_, axis=mybir.AxisListType.X),
                     reads=[b_g1_], writes=[b_st2_])
                yield
                ts(st2_[:, 3:4], st2_[:, 2:3], 1.0 / 512, EPS, ALU.mult, ALU.add, [b_st2_], [b_st2_])
                yield
                act(st2_[:, 3:4], st2_[:, 3:4], AF.Sqrt, [b_st2_], [b_st2_])
                yield
                S.op("dve", lambda e, st2_=st2_: e.reciprocal(out=st2_[:, 3:4], in_=st2_[:, 3:4]), reads=[b_st2_], writes=[b_st2_])
                yield
                stt(v32_, v32_, st2_[:, 3:4], lng_bc, ALU.mult, ALU.mult, [b_v32_, b_st2_, b_bcp], [b_v32_])
                yield
                want_out = is_s or (tile0 + t == 15)
                if want_out:
                    ost, bost = nextof(ostage, "ost")
                    tt(ost[:, :], v32_, lnb_bc, ALU.add, [b_v32_, b_bcp], [bost])
                    S.dma("sp", (vs if is_s else vp)[L], ost[:, :], bost, False)
                    yield
                    cp(vnb[:, t, :], ost[:, :], [bost], [b_vnb], eng="act")
                else:
                    tt(vnb[:, t, :], v32_, lnb_bc, ALU.add, [b_v32_, b_bcp], [b_vnb])
                yield

            for t0_ in range(0, nt, 2):
                gens = [vstage(t) for t in range(t0_, min(t0_ + 2, nt))]
                live = list(gens)
                while live:
                    for g_ in list(live):
                        if next(g_, "END") == "END":
                            live.remove(g_)
            wm_use, b_wm_use = (wmTs, b_wmTs) if is_s else (wmT, b_wmT)
            for g in range(4):
                p_, bp_ = psum()

                def smm(e, p_=p_, g=g, vnb=vnb, wm_use=wm_use, nt=nt):
                    r = None
                    for t in range(nt):
                        r = e.matmul(p_[:, t * 128:(t + 1) * 128], lhsT=vnb[:, t, g * 128:(g + 1) * 128],
                                     rhs=wm_use[:, g, :], start=True, stop=True)
                    return r
                S.op("pe", smm, reads=[b_vnb, b_wm_use], writes=[bp_])
                if is_s:
                    bias_v = bsp_bc[:, g * 128:g * 128 + 8].unsqueeze(1).to_broadcast([128, 16, 8])
                    o3 = g1[:, 0:NT].rearrange("p (j t) -> p j t", t=8)
                    i3 = p_[:, 0:NT].rearrange("p (j t) -> p j t", t=8)
                else:
                    bias_v = bcast_mid(bsp_bc[:, g * 128:(g + 1) * 128], nt)
                    o3 = g1[:, 0:NT].rearrange("p (a t) -> p a t", t=128)
                    i3 = p_[:, 0:NT].rearrange("p (a t) -> p a t", t=128)
                tt(o3, i3, bias_v, ALU.add, [bp_, b_bcp], [b_g1])
                tt(ybT[:, g, 0:NT], g1[:, 0:NT], uT[:, g, :], ALU.mult, [b_g1, b_uT], [b_ybT])
            vpx1, bwp1 = wload([(w_in, O_POOL, 384)])
            vpx2, bwp2 = wload([(w_in, O_POOL + 384, 128)])
            if is_s:
                S.dma("pool", spAb[:], spool[L][0:120, :], b_spAb, True)
                S.dma("pool", spBb[:], spool[L][120:240, :], b_spBb, True)
                S.dma("sp", pools[L][:, 0:7, :], spool[L].rearrange("(j r) c -> j r c", r=15)[:, 8:15, :], b_d2d, False)
            for t in range(nt):
                p_, bp_ = psum("mm")
                mm(p_[:, 0:384], [(hT[:, kc, t * 128:(t + 1) * 128], vpx1[:, kc, :]) for kc in range(8)],
                   [bwp1, b_hT], [bp_])
                mm(p_[:, 384:512], [(hT[:, kc, t * 128:(t + 1) * 128], vpx2[:, kc, :]) for kc in range(8)],
                   [bwp2, b_hT], [bp_])
                cp(xcb[:, t, :], p_[:, :], [bp_], [b_xcb], eng="act")
                if is_s or (tile0 + t == 15):
                    ost, bost = nextof(ostage, "ost")
                    cp(ost[:, :], p_[:, :], [bp_], [bost])
                    if is_s:
                        for j in range(16):
                            S.dma("sp", pools[L][j, 7:15, :], ost[j * 8:(j + 1) * 8, :], bost, False)
                    else:
                        S.dma("sp", poolp[L], ost[113:128, :], bost, False)
            for g in range(4):
                p_, bp_ = psum()

                def dmm(e, p_=p_, g=g, xcb=xcb, is_s=is_s, nt=nt, tile0=tile0):
                    r = None
                    for t in range(nt):
                        gt = tile0 + t
                        o = p_[:, t * 128:(t + 1) * 128]
                        cur = xcb[:, t, g * 128:(g + 1) * 128]
                        if is_s:
                            e.matmul(o, lhsT=cur, rhs=cbf[:, C_DCURS + g, :], start=True, stop=False)
                            e.matmul(o, lhsT=spAb[:, g * 128:(g + 1) * 128], rhs=cbf[0:120, C_DPA + g, :],
                                     start=False, stop=False)
                            r = e.matmul(o, lhsT=spBb[:, g * 128:(g + 1) * 128], rhs=cbf[0:120, C_DPB + g, :],
                                         start=False, stop=True)
                        else:
                            dc = C_DCUR0 if gt == 0 else C_DCUR
                            prev = xcprev[:, g * 128:(g + 1) * 128] if t == 0 else xcb[:, t - 1, g * 128:(g + 1) * 128]
                            e.matmul(o, lhsT=cur, rhs=cbf[:, dc + g, :], start=True, stop=False)
                            r = e.matmul(o, lhsT=prev, rhs=cbf[:, C_DPREV + g, :], start=False, stop=True)
                    return r
                S.op("pe", dmm, reads=[b_xcb, b_xcprev, b_cbf, b_spAb, b_spBb], writes=[bp_])
                cp(dT[:, g, :], p_[:, 0:NT], [bp_], [b_dT], eng=("act" if g % 2 else "dve"))
            if not is_s:
                cp(xcprev[:], xcb[:, nt - 1, :], [b_xcb], [b_xcprev])
            t_pw, b_pw = nextof(slots, "slot")
            pwv = t_pw[:, 0:512].rearrange("p (g d) -> p g d", d=128)
            S.dma("pool", pwv, W["pool_w"][L].rearrange("g c d -> c g d"), b_pw, True)
            for g in range(4):
                p_, bp_ = psum("mm")
                mm(p_[:, 0:NT], [(pwv[:, g, :], dT[:, g, :])], [b_pw, b_dT], [bp_])
                act(ycT[:, g, 0:NT], p_[:, 0:NT], AF.Copy, [bp_, b_pcol], [b_ycT], scale=pc(PC_PS + g))

            if STAGES < 4:
                continue
            R.reset()
            mgT, b_mgT = R.bf("mgT", [128, 8, NT])
            sgt, b_sgt = R.f32("sgt", [128, NT])
            acc, b_acc = R.f32("acc", [128, NT])
            acc2, b_acc2 = R.f32("acc2", [128, NT])
            for j in range(8):
                first = True
                for bi, (wbr, yT, b_y, KCb) in enumerate(((W["w_br_a"][L], yaT, b_yaT, 8),
                                                          (W["w_br_b"][L], ybT, b_ybT, 4),
                                                          (W["w_br_c"][L], ycT, b_ycT, 4))):
                    (vg, vw), bw = wload2([(w_in, O_GATE + bi * 1024 + j * 128, 128), (wbr, j * 128, 128)])
                    pg, bpg = psum("mm")
                    mm(pg[:, 0:NT], [(vg[:, kc, :], hT[:, kc, 0:NT]) for kc in range(8)], [bw, b_hT], [bpg])
                    pb2, bpb2 = psum("mm")
                    mm(pb2[:, 0:NT], [(vw[:, kc, :], yT[:, kc, 0:NT]) for kc in range(KCb)], [bw, b_y], [bpb2])
                    act(sgt, pg[:, 0:NT], AF.Sigmoid, [bpg], [b_sgt])
                    if first:
                        tt(acc, sgt, pb2[:, 0:NT], ALU.mult, [b_sgt, bpb2], [b_acc])
                        first = False
                    else:
                        tt(acc2, sgt, pb2[:, 0:NT], ALU.mult, [b_sgt, bpb2], [b_acc2])
                        tt(acc, acc, acc2, ALU.add, [b_acc, b_acc2], [b_acc])
                cp(mgT[:, j, :], acc, [b_acc], [b_mgT], eng="act")

            def add_x(j, p_, bp_):
                tt(xT[:, j, tok0:tok0 + NT], xT[:, j, tok0:tok0 + NT], p_[:, 0:NT], ALU.add, [b_x, bp_], [b_x])
            for jp in range(3):
                ncb = 3 if jp < 2 else 2
                view, bw = wload([(W["w_out"][L], jp * 384, ncb * 128)])
                for jj in range(ncb):
                    j = jp * 3 + jj
                    p_, bp_ = psum("mm")
                    mm(p_[:, 0:NT], [(view[:, kc, jj * 128:(jj + 1) * 128], mgT[:, kc, :]) for kc in range(8)],
                       [bw, b_mgT], [bp_])
                    add_x(j, p_, bp_)

            if STAGES < 5:
                continue
            R.reset()
            rmsnorm_T(tok0, NT, lambda c: pc(PC_NF + c), hT, b_hT)
            actT, b_actT = R.bf("actT", [128, 22, NT])
            sg2, b_sg2 = R.f32("sg2", [128, NT])
            wgu = W["w_gate_up"][L]
            for fc in range(22):
                view, bw = wload([(wgu, fc * 128, 128), (wgu, D_FF + fc * 128, 128)])
                pg, bpg = psum("mm")
                mm(pg[:, 0:NT], [(view[:, kc, 0:128], hT[:, kc, 0:NT]) for kc in range(8)], [bw, b_hT], [bpg])
                pu, bpu = psum("mm")
                mm(pu[:, 0:NT], [(view[:, kc, 128:256], hT[:, kc, 0:NT]) for kc in range(8)], [bw, b_hT], [bpu])
                act(sg2, pg[:, 0:NT], AF.Silu, [bpg], [b_sg2])
                tt(actT[:, fc, :], sg2, pu[:, 0:NT], ALU.mult, [b_sg2, bpu], [b_actT])
            for j in range(8):
                view, bw = wload([(W["w_down"][L], j * 128, 128)])
                p_, bp_ = psum("mm")
                mm(p_[:, 0:NT], [(view[:, fc, :], actT[:, fc, :]) for fc in range(22)], [bw, b_actT], [bp_])
                add_x(j, p_, bp_)

            if STAGES < 6:
                continue
            R.reset()
            rmsnorm_T(tok0, NT, lambda c: pc(PC_NP + c), hT, b_hT)
            pT, b_pT = R.bf("pT", [128, 2, NT])
            sg3, b_sg3 = R.f32("sg3", [128, NT])
            for t in range(nt):
                pst_, bpst_ = nextof(pstage, "pst")
                S.dma("sp", pst_[:, :], pin[L][tok0 + t * 128:tok0 + (t + 1) * 128, :], bpst_, True)
                p_, bp_ = psum()
                transposes([(p_[:, i * 128:(i + 1) * 128], pst_[:, i * 128:(i + 1) * 128], ident32) for i in range(2)],
                           [bpst_, b_c32], [bp_])
                cp(pT[:, :, t * 128:(t + 1) * 128], p_[:, 0:256].rearrange("p (a b) -> p a b", b=128), [bp_], [b_pT])
            for jp in range(4):
                vg, bg = wload([(W["w_ple_gate"][L], jp * 256, 256)])
                vu_, bu_ = wload([(W["w_ple_up"][L], jp * 256, 256)])
                for jj in range(2):
                    j = jp * 2 + jj
                    cs_ = slice(jj * 128, (jj + 1) * 128)
                    pg, bpg = psum("mm")
                    mm(pg[:, 0:NT], [(vg[:, kc, cs_], hT[:, kc, 0:NT]) for kc in range(8)], [bg, b_hT], [bpg])
                    pu, bpu = psum("mm")
                    mm(pu[:, 0:NT], [(vu_[:, kc, cs_], pT[:, kc, :]) for kc in range(2)], [bu_, b_pT], [bpu])
                    act(sg3, pg[:, 0:NT], AF.Sigmoid, [bpg], [b_sg3])
                    tt(sg3, sg3, pu[:, 0:NT], ALU.mult, [b_sg3, bpu], [b_sg3])
                    tt(xT[:, j, tok0:tok0 + NT], xT[:, j, tok0:tok0 + NT], sg3, ALU.add, [b_x, b_sg3], [b_x])

    for (tile0, nt, is_s) in GROUPS:
        NT = nt * 128
        tok0 = tile0 * 128
        R.reset()
        yT, b_yT = R.f32("yT", [128, 8, NT])
        rmsnorm_T(tok0, NT, lambda c: fcol[:, c:c + 1], yT, b_yT)
        for t in range(nt):
            st, bst = nextof(big, "big")
            for hb in range(2):
                p_, bp_ = psum()
                transposes([(p_[:, j * 128:(j + 1) * 128], yT[:, hb * 4 + j, t * 128:(t + 1) * 128], ident32)
                            for j in range(4)], [b_yT, b_c32], [bp_])
                cp(st[:, hb * 512:(hb + 1) * 512], p_[:, :], [bp_], [bst], eng=("act" if hb else "dve"))
            S.dma("sp", yout[tok0 + t * 128:tok0 + (t + 1) * 128, :], st[:, 0:1024], bst, False)

    S.emit()


_CACHE = {}


def _get_program():
    if "nc" not in _CACHE:
        _CACHE["nc"] = build_program()
    return _CACHE["nc"]


def make_in_maps(inputs):
    f = lambda a: np.ascontiguousarray(np.asarray(a, dtype=np.float32))
    cst = make_consts()
    wnames = ["norm_mix", "w_in", "conv_w", "conv_b", "dt_bias", "a_log", "d_skip", "ssd_norm", "sgu_ln_g",
              "sgu_ln_b", "w_spatial", "b_spatial", "pool_w", "pool_scale", "w_br_a", "w_br_b", "w_br_c", "w_out",
              "norm_ffn", "w_gate_up", "w_down", "norm_ple", "w_ple_gate", "w_ple_up", "final_norm"]
    wts = {n: f(inputs[n]) for n in wnames}
    xp, xs = f(inputs["x_prompt"]), f(inputs["x_sample"])
    pp, psm = f(inputs["p_prompt"]), f(inputs["p_sample"])
    sc, ss, spl = f(inputs["state_conv"]), f(inputs["state_ssm"]), f(inputs["state_pool"])
    maps = []
    for b in range(NCORES):
        sl = slice(16 * b, 16 * b + 16)
        m = dict(wts)
        m["xin"] = np.ascontiguousarray(np.concatenate([xp[b], xs[sl].reshape(128, D)], axis=0))
        m["pin"] = np.ascontiguousarray(np.concatenate([pp[:, b], psm[:, sl].reshape(DEPTH, 128, 256)], axis=1))
        m["sconv"] = np.ascontiguousarray(sc[:, sl].reshape(DEPTH, 48, 1536))
        m["sssm"] = np.ascontiguousarray(ss[:, sl])
        m["spool"] = np.ascontiguousarray(spl[:, sl].reshape(DEPTH, 240, 512))
        m["cst"] = cst
        maps.append(m)
    return maps


def assemble(results):
    B = NCORES
    y_p = np.stack([r["yout"][:2048] for r in results]).astype(np.float32)
    y_s = np.concatenate([r["yout"][2048:].reshape(16, 8, D) for r in results], axis=0).astype(np.float32)
    conv_p = np.stack([r["convp"] for r in results], axis=1)
    ssm_p = np.stack([r["ssmp"].reshape(DEPTH, 16, 64, 128) for r in results], axis=1)
    pool_p = np.stack([r["poolp"] for r in results], axis=1)
    v_p = np.stack([r["vp"] for r in results], axis=1)
    conv_s = np.concatenate([r["convs"].reshape(DEPTH, 16, 3, 1536) for r in results], axis=1)
    ssm_s = np.concatenate([r["ssms"] for r in results], axis=1)
    pool_s = np.concatenate([r["pools"] for r in results], axis=1)
    v_s = np.concatenate([r["vs"].reshape(DEPTH, 16, 8, 512) for r in results], axis=1)
    outs = (y_p, y_s, conv_p, ssm_p, pool_p, v_p, conv_s, ssm_s, pool_s, v_s)
    return tuple(np.ascontiguousarray(o, dtype=np.float32) for o in outs)


def kernel(**inputs):
    nc = _get_program()
    maps = make_in_maps(inputs)
    if DEBUG_CORES:
        res = run_bass_kernel_spmd(nc, maps[:DEBUG_CORES], core_ids=list(range(DEBUG_CORES)), trace=DEBUG_TRACE)
        print("DEBUG exec_time_ns", res.exec_time_ns, flush=True)
        rs = list(res.results)
        return assemble(rs + [rs[0]] * (NCORES - len(rs)))
    res = run_bass_kernel_spmd(nc, maps, core_ids=list(range(NCORES)))
    return assemble(res.results)
```

```python
import math
import numpy as np
import concourse.bass as bass
import concourse.mybir as mybir
from concourse.bass_utils import run_bass_kernel_spmd

F32 = mybir.dt.float32
BF16 = mybir.dt.bfloat16
AF = mybir.ActivationFunctionType
ALU = mybir.AluOpType

NCORES = 8
DEPTH = 4
D = 1024
NTILE = 17
TOK = NTILE * 128
D_FF = 2816
O_Z, O_XBC, O_DT, O_UV, O_POOL, O_GATE = 0, 1024, 2560, 2576, 3600, 4112
IN_DIM = 7184
EPS = 1e-6
GROUPS = [(0, 4, False), (4, 4, False), (8, 4, False), (12, 4, False), (16, 1, True)]
RUN_DEPTH = DEPTH
STAGES = 99
DEBUG_CORES = 0
DEBUG_TRACE = False
SSDLVL = 99
PIPELINE = True
FAST_PERHEAD = True
FAST_GELU = True

ENGS = ("pe", "act", "dve", "pool", "sp")


class Buf:
    __slots__ = ("name", "w", "r", "dsem", "dcnt", "excl")

    def __init__(self, name, excl=False):
        self.name = name
        self.w = None
        self.r = {}
        self.dsem = None
        self.dcnt = 0
        self.excl = excl


class Op:
    __slots__ = ("eng", "fn", "deps", "needed", "tok", "is_dma", "dsem")

    def __init__(self, eng, fn):
        self.eng = eng
        self.fn = fn
        self.deps = []
        self.needed = False
        self.tok = None
        self.is_dma = False
        self.dsem = None


class Sched:
    def __init__(self, nc):
        self.nc = nc
        self.ops = {e: [] for e in ENGS}
        self.all_ops = []
        self.dma_bufs = []
        self.nsem = 0
        self.last = {e: None for e in ENGS}
        self.pending = {e: [] for e in ENGS}

    def new_sem(self, name):
        self.nsem += 1
        return self.nc.alloc_semaphore(name)

    def barrier(self, engines=("pe", "act", "dve")):
        for e in engines:
            for o in engines:
                if o != e and self.last[o] is not None:
                    self.pending[e].append(self.last[o])

    def _track(self, rec, reads, writes):
        deps = []
        for b in reads:
            if b.w is not None:
                deps.append(b.w)
            if b.excl:
                deps.extend(v for k_, v in b.r.items() if k_ != rec.eng)
        for b in writes:
            if b.w is not None:
                deps.append(b.w)
            deps.extend(b.r.values())
        if self.pending[rec.eng]:
            deps.extend(self.pending[rec.eng])
            self.pending[rec.eng] = []
        seen = set()
        out = []
        for d in deps:
            if d is rec or id(d) in seen:
                continue
            if rec.eng == "pe" and d.eng == "pe" and not d.is_dma and not rec.is_dma:
                continue
            seen.add(id(d))
            out.append(d)
            d.needed = True
        rec.deps = out
        key = rec.eng + ("_dma" if rec.is_dma else "")
        for b in reads:
            b.r[key] = rec
        for b in writes:
            b.w = rec
            b.r = {}
        self.ops[rec.eng].append(rec)
        self.all_ops.append(rec)
        if not rec.is_dma:
            self.last[rec.eng] = rec

    def op(self, eng, fn, reads=(), writes=()):
        rec = Op(eng, fn)
        self._track(rec, reads, writes)
        return rec

    def dma(self, eng, out_ap, in_ap, sbuf, is_load, extra_reads=(), cont=False, **kw):
        rec = Op(eng, lambda e: e.dma_start(out=out_ap, in_=in_ap, **kw))
        rec.is_dma = True
        if sbuf.dsem is None:
            sbuf.dsem = self.new_sem("d_" + sbuf.name)
            self.dma_bufs.append(sbuf)
        sbuf.dcnt += 16
        rec.dsem = sbuf.dsem
        rec.tok = (sbuf.dsem, sbuf.dcnt)
        if is_load and cont:
            assert sbuf.w is not None and sbuf.w.is_dma and sbuf.w.eng == eng and not sbuf.r
            rec.deps = []
            sbuf.w = rec
            self.ops[eng].append(rec)
            self.all_ops.append(rec)
        elif is_load:
            self._track(rec, list(extra_reads), [sbuf])
        else:
            self._track(rec, [sbuf] + list(extra_reads), [])
        return rec

    def emit(self):
        nc = self.nc
        esem = {e: self.new_sem("e_" + e) for e in ENGS}
        cnt = {e: 0 for e in ENGS}
        for rec in self.all_ops:
            if rec.is_dma:
                continue
            if rec.needed:
                cnt[rec.eng] += 1
                rec.tok = (esem[rec.eng], cnt[rec.eng])
        final_waits = [(b.dsem, b.dcnt) for b in self.dma_bufs]
        ops = self.ops

        def run(eng_name, e):
            waited = {}
            for rec in ops[eng_name]:
                for d in rec.deps:
                    sem, val = d.tok
                    k = sem.num
                    if waited.get(k, 0) < val:
                        e.wait_ge(sem, val)
                        waited[k] = val
                ins = rec.fn(e)
                if rec.is_dma:
                    ins.then_inc(rec.dsem, 16)
                elif rec.needed:
                    ins.then_inc(esem[eng_name], 1)
            if eng_name == "sp":
                for sem, val in final_waits:
                    if waited.get(sem.num, 0) < val:
                        e.wait_ge(sem, val)

        with nc.Block() as block:
            @block.tensor
            def _(e):
                run("pe", e)

            @block.scalar
            def _(e):
                run("act", e)

            @block.vector
            def _(e):
                run("dve", e)

            @block.gpsimd
            def _(e):
                run("pool", e)

            @block.sync
            def _(e):
                run("sp", e)
        print("sched: ops", {k: len(v) for k, v in ops.items()}, "signals", cnt, "nsem", self.nsem, flush=True)


C_ID, C_ONES, C_U, C_NEGU, C_CM, C_US, C_OS, C_IND = range(8)
C_DCUR0, C_DCUR, C_DPREV, C_DCURS, C_DPA, C_DPB = 8, 12, 16, 20, 24, 28
NCST = 32
POOL_W = (2, 4, 8, 16)


def make_consts():
    c = np.zeros((NCST, 128, 128), np.float32)
    k = np.arange(128)[:, None]
    l = np.arange(128)[None, :]
    c[C_ID] = (k == l)
    c[C_ONES] = 1.0
    c[C_U] = (k <= l)
    c[C_NEGU] = -(k <= l).astype(np.float32)
    c[C_CM] = 3.0e4 * (l < k)
    same = (k // 8) == (l // 8)
    c[C_US] = (k <= l) & same
    c[C_OS] = same
    c[C_IND][:, :16] = (k // 8) == np.arange(16)[None, :]
    s, t = k, l
    for g, w in enumerate(POOL_W):
        band = ((s <= t) & (s > t - w)).astype(np.float32)
        cnt0 = np.minimum(np.arange(128) + 1, w).astype(np.float32)[None, :]
        c[C_DCUR0 + g] = band / cnt0 - (s == t)
        c[C_DCUR + g] = band / w - (s == t)
        c[C_DPREV + g] = (s > 128 + t - w).astype(np.float32) / w
        si, ti = s % 8, t % 8
        c[C_DCURS + g] = (same & (si <= ti) & (si > ti - w)).astype(np.float32) / w - (s == t)
        for half, idx in ((0, C_DPA), (1, C_DPB)):
            m = np.zeros((128, 128), np.float32)
            for row in range(120):
                j = row // 15 + 8 * half
                r = row % 15
                for col in range(128):
                    if col // 8 == j and r > 15 + (col % 8) - w:
                        m[row, col] = 1.0 / w
            c[idx + g] = m
    return c


def build_program():
    nc = bass.Bass("TRN2", target_bir_lowering=False)
    with nc.cleanup_on_exit():
        _build_body(nc)
        nc.all_engine_barrier()
    return nc


def _build_body(nc):
    S = Sched(nc)

    def din(name, shape):
        return nc.dram_tensor(name, list(shape), F32, kind="ExternalInput").ap()

    def dout(name, shape):
        return nc.dram_tensor(name, list(shape), F32, kind="ExternalOutput").ap()

    xin = din("xin", [TOK, D])
    pin = din("pin", [DEPTH, TOK, 256])
    sconv = din("sconv", [DEPTH, 48, 1536])
    sssm = din("sssm", [DEPTH, 16, 16, 64, 128])
    spool = din("spool", [DEPTH, 240, 512])
    cst = din("cst", [NCST, 128, 128])
    W = {}
    for name, shape in [
        ("norm_mix", [DEPTH, D]), ("w_in", [DEPTH, D, IN_DIM]), ("conv_w", [DEPTH, 4, 1536]),
        ("conv_b", [DEPTH, 1536]), ("dt_bias", [DEPTH, 16]), ("a_log", [DEPTH, 16]), ("d_skip", [DEPTH, 16]),
        ("ssd_norm", [DEPTH, D]), ("sgu_ln_g", [DEPTH, 512]), ("sgu_ln_b", [DEPTH, 512]),
        ("w_spatial", [DEPTH, 4, 128, 128]), ("b_spatial", [DEPTH, 4, 128]), ("pool_w", [DEPTH, 4, 128, 128]),
        ("pool_scale", [DEPTH, 512]), ("w_br_a", [DEPTH, D, D]), ("w_br_b", [DEPTH, 512, D]),
        ("w_br_c", [DEPTH, 512, D]), ("w_out", [DEPTH, D, D]), ("norm_ffn", [DEPTH, D]),
        ("w_gate_up", [DEPTH, D, 2 * D_FF]), ("w_down", [DEPTH, D_FF, D]), ("norm_ple", [DEPTH, D]),
        ("w_ple_gate", [DEPTH, D, D]), ("w_ple_up", [DEPTH, 256, D]), ("final_norm", [D]),
    ]:
        W[name] = din(name, shape)
    yout = dout("yout", [TOK, D])
    convp = dout("convp", [DEPTH, 3, 1536])
    ssmp = dout("ssmp", [DEPTH, 1024, 128])
    poolp = dout("poolp", [DEPTH, 15, 512])
    vp = dout("vp", [DEPTH, 128, 512])
    convs = dout("convs", [DEPTH, 48, 1536])
    ssms = dout("ssms", [DEPTH, 16, 16, 64, 128])
    pools = dout("pools", [DEPTH, 16, 15, 512])
    vs = dout("vs", [DEPTH, 128, 512])

    def sb(name, shape, dt=F32):
        return nc.alloc_sbuf_tensor(name, list(shape), dt), Buf(name)

    xT, b_x = sb("xT", [128, 8, TOK])
    c32, b_c32 = sb("c32", [128, 8, 128])
    cbf, b_cbf = sb("cbf", [128, NCST, 128], BF16)
    pcol, b_pcol = sb("pcol", [128, DEPTH, 96])
    fcol, b_fcol = sb("fcol", [128, 8])
    ptile, b_ptile = sb("ptile", [96, 128])
    bcp, b_bcp = sb("bcp", [128, 1584])
    NSLOT = 4
    SLOTN = 3072
    slots = [sb(f"wslot{i}", [128, SLOTN], BF16) for i in range(NSLOT)]
    hT, b_hT = sb("hT", [128, 8, 512], BF16)
    yaT, b_yaT = sb("yaT", [128, 8, 512], BF16)
    halo, b_halo = sb("halo", [128, 12, 3])
    xcprev, b_xcprev = sb("xcprev", [128, 512], BF16)
    prevT32, b_prev32 = sb("prevT32", [128, 1024])
    prevTb, b_prevb = sb("prevTb", [128, 1024], BF16)
    big = [sb(f"big{i}", [128, 2048]) for i in range(1)]
    pstage = [sb(f"pstage{i}", [128, 256]) for i in range(1)]
    ostage = [sb(f"ostage{i}", [128, 512]) for i in range(2)]
    cstage, b_cstage = big[0][0][0:48, 0:768], big[0][1]
    b_d2d = Buf("d2d")
    spAb, b_spAb = sb("spAb", [120, 512], BF16)
    spBb, b_spBb = sb("spBb", [120, 512], BF16)
    w8, b_w8 = sb("w8", [128, 4, 8])
    wsp, b_wsp = sb("wsp", [128, 4, 128])
    wmT, b_wmT = sb("wmT", [128, 4, 128], BF16)
    wmTs, b_wmTs = sb("wmTs", [128, 4, 128], BF16)
    REGN = 12800
    reg, _ = sb("reg", [128, REGN])

    ps = []
    for i in range(8):
        ps.append((nc.alloc_psum_tensor(f"ps{i}", [128, 512], F32), Buf(f"ps{i}", excl=True)))
    rr = {"mm": 0, "aux": 0, "slot": 0, "big": 0, "ost": 0, "pst": 0}

    def psum(pool="aux"):
        if pool == "mm":
            i = rr["mm"] % 4
            rr["mm"] += 1
            return ps[i]
        if pool == "long0":
            return ps[4]
        if pool == "long1":
            return ps[5]
        i = 6 + rr["aux"] % 2
        rr["aux"] += 1
        return ps[i]

    def nextof(lst, key):
        i = rr[key] % len(lst)
        rr[key] += 1
        return lst[i]

    class Region:
        def __init__(self, off=0, lim=None):
            self.off = off
            self.lim = REGN if lim is None else lim

        def sub(self, off, size):
            return Region(off, off + size)

        def reset(self):
            self.off = 0
            S.barrier()

        def f32(self, name, shape):
            n = int(np.prod(shape[1:]))
            assert self.off + n <= self.lim, (name, self.off, n)
            ap = reg[:, self.off:self.off + n]
            self.off += n
            return self._shape(ap, shape), Buf(name)

        def bf(self, name, shape):
            n = int(np.prod(shape[1:]))
            n32 = (n + 1) // 2
            assert self.off + n32 <= self.lim, (name, self.off, n32)
            ap = reg[:, self.off:self.off + n32].bitcast(BF16)[:, 0:n]
            self.off += n32
            return self._shape(ap, shape), Buf(name)

        @staticmethod
        def _shape(ap, shape):
            if len(shape) == 2:
                return ap
            if len(shape) == 3:
                return ap.rearrange("p (a b) -> p a b", b=shape[2])
            if len(shape) == 4:
                return ap.rearrange("p (a b c) -> p a b c", b=shape[2], c=shape[3])
            raise ValueError

    R = Region()
    ybT = reg[:, REGN - 2048:REGN - 1024].bitcast(BF16).rearrange("p (a b) -> p a b", b=512)
    ycT = reg[:, REGN - 1024:REGN].bitcast(BF16).rearrange("p (a b) -> p a b", b=512)
    b_ybT, b_ycT = Buf("ybT"), Buf("ycT")

    def act(out, in_, func, reads, writes, bias=0.0, scale=1.0, accum_out=None):
        kw = {}
        if accum_out is not None:
            kw["accum_out"] = accum_out
        return S.op("act", lambda e: e.activation(out=out, in_=in_, func=func, bias=bias, scale=scale, **kw),
                    reads=reads, writes=writes)

    def tt(out, in0, in1, op, reads, writes, eng="dve"):
        return S.op(eng, lambda e: e.tensor_tensor(out=out, in0=in0, in1=in1, op=op), reads=reads, writes=writes)

    def ts(out, in0, s1, s2, op0, op1, reads, writes, eng="dve", accum_out=None):
        kw = {}
        if accum_out is not None:
            kw["accum_out"] = accum_out
        if op1 is None:
            return S.op(eng, lambda e: e.tensor_scalar(out=out, in0=in0, scalar1=s1, scalar2=None, op0=op0, **kw),
                        reads=reads, writes=writes)
        return S.op(eng, lambda e: e.tensor_scalar(out=out, in0=in0, scalar1=s1, scalar2=s2, op0=op0, op1=op1, **kw),
                    reads=reads, writes=writes)

    def stt(out, in0, scalar, in1, op0, op1, reads, writes, eng="dve"):
        return S.op(eng, lambda e: e.scalar_tensor_tensor(out=out, in0=in0, scalar=scalar, in1=in1, op0=op0, op1=op1),
                    reads=reads, writes=writes)

    def cp(out, in_, reads, writes, eng="dve"):
        if eng == "act":
            return act(out, in_, AF.Copy, reads, writes)
        return S.op(eng, lambda e: e.tensor_copy(out=out, in_=in_), reads=reads, writes=writes)

    def memset(ap, val, writes, eng="dve"):
        return S.op(eng, lambda e: e.memset(ap, val), writes=writes)

    def mm(out, pairs, reads, writes):
        def fn(e):
            r = None
            n = len(pairs)
            for i, (l, rh) in enumerate(pairs):
                r = e.matmul(out, lhsT=l, rhs=rh, start=(i == 0), stop=(i == n - 1))
            return r
        return S.op("pe", fn, reads=reads, writes=writes)

    def transposes(items, reads, writes):
        def fn(e):
            r = None
            for o, i, idn in items:
                r = e.transpose(o, i, idn)
            return r
        return S.op("pe", fn, reads=reads, writes=writes)

    ident32 = c32[:, C_ID, :]
    identb = cbf[:, C_ID, :]
    onesb = cbf[:, C_ONES, :]

    def bcast_mid(ap2d, n):
        return ap2d.unsqueeze(1).to_broadcast([128, n, ap2d.shape[1]])

    def perhead(eng, out2d, in2d, vec, op, reads, writes, in1_2d=None, op1=None):
        def v3(a):
            return a[:, 0:512].rearrange("p (h d) -> p h d", d=64)
        vb = vec.unsqueeze(2).to_broadcast([128, 8, 64])
        if not FAST_PERHEAD:
            def fn(e):
                r = None
                for h in range(8):
                    sl = slice(h * 64, (h + 1) * 64)
                    if eng == "act":
                        r = e.activation(out=out2d[:, sl], in_=in2d[:, sl], func=AF.Copy, scale=vec[:, h:h + 1])
                    elif in1_2d is None:
                        r = e.tensor_scalar(out=out2d[:, sl], in0=in2d[:, sl], scalar1=vec[:, h:h + 1], scalar2=None, op0=op)
                    else:
                        r = e.scalar_tensor_tensor(out=out2d[:, sl], in0=in2d[:, sl], scalar=vec[:, h:h + 1],
                                                   in1=in1_2d[:, sl], op0=op, op1=op1)
                return r
            return S.op(eng, fn, reads=reads, writes=writes)

        def fn(e):
            r = e.tensor_tensor(out=v3(out2d), in0=v3(in2d), in1=vb, op=op)
            if in1_2d is not None:
                r = e.tensor_tensor(out=v3(out2d), in0=v3(out2d), in1=v3(in1_2d), op=op1)
            return r
        return S.op("dve", fn, reads=reads, writes=writes)

    def hdv(ap2d, d=64):
        return ap2d.rearrange("p (h d) -> p d h", d=d)

    def bc_hd(vec, d=64):
        return vec.unsqueeze(1).to_broadcast([128, d, vec.shape[1]])

    def wload(parts):
        t, b = nextof(slots, "slot")
        K = parts[0][0].shape[0]
        KC = K // 128
        tot = sum(p[2] for p in parts)
        assert KC * tot <= SLOTN
        view = t[:, 0:KC * tot].rearrange("p (c f) -> p c f", f=tot)
        o = 0
        for i, (w2d, c0, ncol) in enumerate(parts):
            src = w2d.rearrange("(c p) f -> p c f", p=128)[:, :, c0:c0 + ncol]
            S.dma("pool", view[:, :, o:o + ncol], src, b, True, cont=(i > 0))
            o += ncol
        return view, b

    def wload2(parts):
        t, b = nextof(slots, "slot")
        o = 0
        views = []
        for i, (w2d, c0, ncol) in enumerate(parts):
            KC = w2d.shape[0] // 128
            n = KC * ncol
            assert o + n <= SLOTN
            view = t[:, o:o + n].rearrange("p (c f) -> p c f", f=ncol)
            S.dma("pool", view, w2d.rearrange("(c p) f -> p c f", p=128)[:, :, c0:c0 + ncol], b, True, cont=(i > 0))
            views.append(view)
            o += n
        return views, b

    cview = cst.rearrange("c p f -> p c f")
    S.dma("sp", c32[:], cview[:, 0:8, :], b_c32, True)
    S.dma("pool", cbf[:], cview, b_cbf, True)
    memset(halo[:], 0.0, [b_halo])

    PC_NM, PC_NF, PC_NP, PC_SN, PC_CW, PC_CB, PC_PS = 0, 8, 16, 24, 32, 80, 92
    for L in range(RUN_DEPTH):
        for (nm, r0, nr) in (("norm_mix", PC_NM, 8), ("norm_ffn", PC_NF, 8), ("norm_ple", PC_NP, 8),
                             ("ssd_norm", PC_SN, 8), ("conv_b", PC_CB, 12), ("pool_scale", PC_PS, 4)):
            S.dma("sp", ptile[r0:r0 + nr, :], W[nm][L].rearrange("(c p) -> c p", p=128), b_ptile, True)
        S.dma("sp", ptile[PC_CW:PC_CW + 48, :], W["conv_w"][L].rearrange("k (c p) -> (k c) p", p=128), b_ptile, True)
        pt, bpt = psum()
        transposes([(pt[:, 0:96], ptile[:, :], ident32[0:96, 0:96])], [b_ptile, b_c32], [bpt])
        cp(pcol[:, L, :], pt[:, 0:96], [bpt], [b_pcol])
    S.dma("sp", ptile[0:8, :], W["final_norm"].rearrange("(c p) -> c p", p=128), b_ptile, True)
    pt, bpt = psum()
    transposes([(pt[:, 0:8], ptile[0:8, :], ident32[0:8, 0:8])], [b_ptile, b_c32], [bpt])
    cp(fcol[:], pt[:, 0:8], [bpt], [b_fcol])

    for t in range(NTILE):
        st, bst = nextof(big, "big")
        S.dma("sp", st[:, 0:1024], xin[t * 128:(t + 1) * 128, :], bst, True)
        for hb in range(2):
            p_, bp_ = psum()
            transposes([(p_[:, j * 128:(j + 1) * 128], st[:, (hb * 4 + j) * 128:(hb * 4 + j + 1) * 128], ident32)
                        for j in range(4)], [bst, b_c32], [bp_])
            cp(xT[:, hb * 4:hb * 4 + 4, t * 128:(t + 1) * 128], p_[:].rearrange("p (a b) -> p a b", b=128),
               [bp_], [b_x], eng=("act" if hb else "dve"))

    def rmsnorm_T(tok0, NT, gcol_of, dst, b_dst, out_is_f32=False):
        sq = [R.bf(f"sq{i}", [128, NT]) for i in range(2)]
        ms, b_ms = R.f32("ms", [128, NT])
        pss, bpss = psum("long0")
        for c in range(8):
            q, bq = sq[c % 2]
            act(q, xT[:, c, tok0:tok0 + NT], AF.Square, [b_x], [bq])
            S.op("pe", lambda e, q=q, c=c: e.matmul(pss[:, 0:NT], lhsT=onesb, rhs=q, start=(c == 0), stop=(c == 7)),
                 reads=[bq, b_cbf], writes=[bpss])
        ts(ms, pss[:, 0:NT], 1.0 / D, EPS, ALU.mult, ALU.add, [bpss], [b_ms])
        act(ms, ms, AF.Sqrt, [b_ms], [b_ms])
        S.op("dve", lambda e: e.reciprocal(out=ms, in_=ms), reads=[b_ms], writes=[b_ms])
        for c in range(8):
            stt(dst[:, c, 0:NT], xT[:, c, tok0:tok0 + NT], gcol_of(c), ms, ALU.mult, ALU.mult,
                [b_x, b_ms, b_pcol, b_fcol], [b_dst])

    def linear_fm(wparts, actT, b_act, KC, NT, nchunks, consume, pool="mm"):
        view, bw = wload(wparts)
        for j in range(nchunks):
            p_, bp_ = psum(pool)
            mm(p_[:, 0:NT], [(view[:, kc, j * 128:(j + 1) * 128], actT[:, kc, 0:NT]) for kc in range(KC)],
               [bw, b_act], [bp_])
            consume(j, p_, bp_)

    for L in range(RUN_DEPTH):
        w_in = W["w_in"][L]
        pc = lambda col: pcol[:, L, col:col + 1]
        for (nm, o, n) in (("dt_bias", 0, 16), ("a_log", 16, 16), ("d_skip", 32, 16), ("sgu_ln_g", 48, 512),
                           ("sgu_ln_b", 560, 512)):
            S.dma("sp", bcp[:, o:o + n], W[nm][L:L + 1, :].to_broadcast([128, n]), b_bcp, True)
        S.dma("sp", bcp[:, 1072:1584], W["b_spatial"][L:L + 1].rearrange("o g t -> o (g t)").to_broadcast([128, 512]),
              b_bcp, True)
        act(bcp[:, 16:32], bcp[:, 16:32], AF.Exp, [b_bcp], [b_bcp])
        ts(bcp[:, 16:32], bcp[:, 16:32], -1.0, None, ALU.mult, None, [b_bcp], [b_bcp])
        dtb_bc, a_bc, dsk_bc = bcp[:, 0:16], bcp[:, 16:32], bcp[:, 32:48]
        lng_bc, lnb_bc, bsp_bc = bcp[:, 48:560], bcp[:, 560:1072], bcp[:, 1072:1584]
        S.dma("sp", wsp[:], W["w_spatial"][L].rearrange("g t s -> t g s"), b_wsp, True)
        pw, bpw = psum()
        transposes([(pw[:, g * 128:(g + 1) * 128], wsp[:, g, :], ident32) for g in range(4)], [b_wsp, b_c32], [bpw])
        tt(wmT[:], pw[:].rearrange("p (g t) -> p g t", t=128), bcast_mid(c32[:, C_U, :], 4), ALU.mult,
           [bpw, b_c32], [b_wmT])
        for j in range(16):
            S.dma("sp", w8[j * 8:(j + 1) * 8, :, :], W["w_spatial"][L][:, 0:8, 0:8].rearrange("g t s -> t g s"),
                  b_w8, True)
        for g in range(4):
            cp(wsp[:, g, :].rearrange("p (j s) -> p j s", s=8), bcast_mid(w8[:, g, :], 16), [b_w8], [b_wsp])
        pw, bpw = psum()
        transposes([(pw[:, g * 128:(g + 1) * 128], wsp[:, g, :], ident32) for g in range(4)], [b_wsp, b_c32], [bpw])
        tt(wmTs[:], pw[:].rearrange("p (g t) -> p g t", t=128), bcast_mid(c32[:, C_US, :], 4), ALU.mult,
           [bpw, b_c32], [b_wmTs])
        memset(prevT32[:], 0.0, [b_prev32])
        memset(prevTb[:], 0.0, [b_prevb])
        memset(halo[:], 0.0, [b_halo])
        memset(xcprev[:], 0.0, [b_xcprev])

        for (tile0, nt, is_s) in GROUPS:
            NT = nt * 128
            tok0 = tile0 * 128
            last_prompt = (tile0 + nt == 16)
            R.reset()
            rmsnorm_T(tok0, NT, lambda c: pc(PC_NM + c), hT, b_hT)
            for sg in (range(2) if STAGES >= 2 else []):
                R.reset()
                zs, b_zs = R.f32("zs", [128, nt, 512])
                xc_off = R.off
                if is_s:
                    xcin, b_xcin = R.f32("xcin", [128, 6, 16, 11])
                else:
                    xcin, b_xcin = R.f32("xcin", [128, 6, NT + 3])
                xc_size = R.off - xc_off
                xsT, b_xsT = R.f32("xsT", [128, 4, NT])
                BT, b_BT = R.bf("BT", [128, NT])
                CT, b_CT = R.bf("CT", [128, NT])
                dtr, b_dtr = R.f32("dtr", [128, nt, 8])
                dtt, b_dtt = R.f32("dtt", [128, nt, 8])
                daa, b_daa = R.f32("daa", [128, nt, 8])
                tmp8, b_tmp8 = R.f32("tmp8", [128, nt, 8])
                cacc, b_cacc = R.f32("cacc", [128, NT])
                if is_s:
                    ctmp, b_ctmp = R.f32("ctmp", [128, 6, 48])
                vz1, bwz1 = wload([(w_in, O_Z + sg * 512, 384)])
                vz2, bwz2 = wload([(w_in, O_Z + sg * 512 + 384, 128)])
                for t in range(nt):
                    p_, bp_ = psum("mm")
                    mm(p_[:, 0:384], [(hT[:, kc, t * 128:(t + 1) * 128], vz1[:, kc, :]) for kc in range(8)],
                       [bwz1, b_hT], [bp_])
                    mm(p_[:, 384:512], [(hT[:, kc, t * 128:(t + 1) * 128], vz2[:, kc, :]) for kc in range(8)],
                       [bwz2, b_hT], [bp_])
                    act(zs[:, t, :], p_[:, :], AF.Silu, [bp_], [b_zs])

                def xc_dst(j):
                    if is_s:
                        return xcin[:, j, :, 3:11]
                    return xcin[:, j, 3:3 + NT]

                def ps_src(p_):
                    if is_s:
                        return p_[:, 0:NT].rearrange("p (j t) -> p j t", t=8)
                    return p_[:, 0:NT]

                vx1, bwx1 = wload([(w_in, O_XBC + sg * 512, 384)])
                vx2, bwx2 = wload([(w_in, O_XBC + sg * 512 + 384, 128)])
                for j in range(4):
                    p_, bp_ = psum("mm")
                    vv_, bb_, jj_ = (vx1, bwx1, j) if j < 3 else (vx2, bwx2, 0)
                    mm(p_[:, 0:NT], [(vv_[:, kc, jj_ * 128:(jj_ + 1) * 128], hT[:, kc, 0:NT]) for kc in range(8)],
                       [bb_, b_hT], [bp_])
                    cp(xc_dst(j), ps_src(p_), [bp_], [b_xcin], eng=("act" if j % 2 else "dve"))
                vbc, bwbc = wload([(w_in, O_XBC + 1024 + sg * 128, 128), (w_in, O_XBC + 1280 + sg * 128, 128),
                                   (w_in, O_DT + sg * 8, 8)])
                for j in range(4, 6):
                    p_, bp_ = psum("mm")
                    o = (j - 4) * 128
                    mm(p_[:, 0:NT], [(vbc[:, kc, o:o + 128], hT[:, kc, 0:NT]) for kc in range(8)],
                       [bwbc, b_hT], [bp_])
                    cp(xc_dst(j), ps_src(p_), [bp_], [b_xcin], eng=("act" if j % 2 else "dve"))
                for t in range(nt):
                    p_, bp_ = psum("mm")
                    mm(p_[:, 0:8], [(hT[:, kc, t * 128:(t + 1) * 128], vbc[:, kc, 256:264]) for kc in range(8)],
                       [bwbc, b_hT], [bp_])
                    cp(dtr[:, t, :], p_[:, 0:8], [bp_], [b_dtr])
                if SSDLVL < 2:
                    continue
                chunk_ids = [sg * 4 + j for j in range(4)] + [8 + sg, 10 + sg]
                if is_s:
                    S.dma("sp", cstage[:, 0:512], sconv[L][:, sg * 512:(sg + 1) * 512], b_cstage, True)
                    S.dma("sp", cstage[:, 512:640], sconv[L][:, 1024 + sg * 128:1024 + (sg + 1) * 128], b_cstage, True)
                    S.dma("sp", cstage[:, 640:768], sconv[L][:, 1280 + sg * 128:1280 + (sg + 1) * 128], b_cstage, True)
                    for jj, cid in enumerate(chunk_ids):
                        p_, bp_ = psum()
                        transposes([(p_[:, 0:48], cstage[:, jj * 128:(jj + 1) * 128], ident32[0:48, 0:48])],
                                   [b_cstage, b_c32], [bp_])
                        cp(xcin[:, jj, :, 0:3], p_[:, 0:48].rearrange("p (j r) -> p j r", r=3), [bp_], [b_xcin])
                else:
                    for jj, cid in enumerate(chunk_ids):
                        cp(xcin[:, jj, 0:3], halo[:, cid, :], [b_halo], [b_xcin], eng="act")
                if last_prompt or is_s:
                    ncol = 48 if is_s else 3
                    for half in range(2):
                        ost, bost = nextof(ostage, "ost")
                        p_, bp_ = psum()
                        items = []
                        for q in range(3):
                            jj = half * 3 + q
                            if is_s:
                                cp(ctmp[:, jj, :].rearrange("p (j r) -> p j r", r=3), xcin[:, jj, :, 8:11],
                                   [b_xcin], [b_ctmp])
                                src = ctmp[:, jj, :]
                            else:
                                src = xcin[:, jj, NT:NT + 3]
                            items.append((p_[0:ncol, q * 128:(q + 1) * 128], src, ident32))
                        transposes(items, [b_xcin, b_c32] + ([b_ctmp] if is_s else []), [bp_])
                        cp(ost[0:ncol, 0:384], p_[0:ncol, 0:384], [bp_], [bost])
                        for q in range(3):
                            cid = chunk_ids[half * 3 + q]
                            dst = (convs if is_s else convp)[L][:, cid * 128:(cid + 1) * 128]
                            S.dma("sp", dst, ost[0:ncol, q * 128:(q + 1) * 128], bost, False)
                if not is_s:
                    for jj, cid in enumerate(chunk_ids):
                        cp(halo[:, cid, :], xcin[:, jj, NT:NT + 3], [b_xcin], [b_halo], eng="act")
                if SSDLVL < 3:
                    continue
                for jj, cid in enumerate(chunk_ids):
                    def win(k):
                        if is_s:
                            return xcin[:, jj, :, k:k + 8]
                        return xcin[:, jj, k:k + NT]
                    cacc_v = cacc.rearrange("p (j t) -> p j t", t=8) if is_s else cacc
                    ts(cacc_v, win(0), pc(PC_CW + 0 * 12 + cid), None, ALU.mult, None, [b_xcin, b_pcol], [b_cacc])
                    for k in range(1, 4):
                        stt(cacc_v, win(k), pc(PC_CW + k * 12 + cid), cacc_v, ALU.mult, ALU.add,
                            [b_xcin, b_pcol, b_cacc], [b_cacc])
                    if jj < 4:
                        dst, bd = xsT[:, jj, :], b_xsT
                    elif jj == 4:
                        dst, bd = BT, b_BT
                    else:
                        dst, bd = CT, b_CT
                    act(dst, cacc, AF.Silu, [b_cacc, b_pcol], [bd], bias=pc(PC_CB + cid))
                if SSDLVL < 4:
                    continue
                dtb = bcast_mid(dtb_bc[:, sg * 8:sg * 8 + 8], nt)
                tt(dtr, dtr, dtb, ALU.add, [b_dtr, b_bcp], [b_dtr])
                act(tmp8, dtr, AF.Abs, [b_dtr], [b_tmp8])
                act(tmp8, tmp8, AF.Exp, [b_tmp8], [b_tmp8], scale=-1.0)
                act(tmp8, tmp8, AF.Ln, [b_tmp8], [b_tmp8], bias=1.0)
                stt(dtt, dtr, 0.0, tmp8, ALU.max, ALU.add, [b_dtr, b_tmp8], [b_dtt])
                tt(daa, dtt, bcast_mid(a_bc[:, sg * 8:sg * 8 + 8], nt), ALU.mult, [b_dtt, b_bcp], [b_daa])
                if SSDLVL < 5:
                    continue
                Ucs = c32[:, C_US, :] if is_s else c32[:, C_U, :]
                Otot = c32[:, C_OS, :] if is_s else c32[:, C_ONES, :]
                cs16, b_cs16 = R.f32("cs16", [128, 16])
                ecs, b_ecs = R.f32("ecs", [128, 8])
                dec, b_dec = R.f32("dec", [128, 8])
                edec, b_edec = R.f32("edec", [128, 8])
                dtdec, b_dtdec = R.f32("dtdec", [128, 8])
                xs32, b_xs32 = R.f32("xs32", [128, 512])
                Xd, b_Xd = R.bf("Xd", [128, 512])
                Xdd, b_Xdd = R.bf("Xdd", [128, 512])
                Btok, b_Btok = R.bf("Btok", [128, 128])
                CBm, b_CBm = R.f32("CBm", [128, 128])
                R1, b_R1 = R.bf("R1", [128, 8, 128])
                R2, b_R2 = R.bf("R2", [128, 8, 128])
                Lt, b_Lt = R.bf("Lt", [128, 8, 128])
                MT, b_MT = R.bf("MT", [128, 8, 128])
                t1, b_t1 = R.f32("t1", [128, 512])
                t2, b_t2 = R.f32("t2", [128, 512])
                ss2, b_ss2 = R.f32("ss2", [128, 2])
                if is_s:
                    H0T, b_H0T = R.bf("H0T", [128, 16, 128])
                    CTpad, b_CTpad = R.bf("CTpad", [128, 16, 128])
                    Bmask, b_Bmask = R.bf("Bmask", [128, 16, 128])
                    daexp, b_daexp = R.f32("daexp", [128, 2, 64])
                    Sc, b_Sc = R.f32("Sc", [128, 16])
                    memset(CTpad[:], 0.0, [b_CTpad])
                hand = [dict(ecs=(ecs, b_ecs), edec=(edec, b_edec), xs32=(xs32, b_xs32), Xd=(Xd, b_Xd),
                             Xdd=(Xdd, b_Xdd), Btok=(Btok, b_Btok), MT=(MT, b_MT))]
                if nt > 1:
                    S.barrier()
                    Rb = R.sub(xc_off, xc_size)
                    hand.append(dict(ecs=Rb.f32("ecsb", [128, 8]), edec=Rb.f32("edecb", [128, 8]),
                                     xs32=Rb.f32("xs32b", [128, 512]), Xd=Rb.bf("Xdb", [128, 512]),
                                     Xdd=Rb.bf("Xddb", [128, 512]), Btok=Rb.bf("Btokb", [128, 128]),
                                     MT=Rb.bf("MTb", [128, 8, 128])))

                batched = nt > 1
                if batched:
                    csA, b_csA = Rb.f32("csA", [128, nt, 16])
                    ecsA, b_ecsA = Rb.f32("ecsA", [128, nt, 8])
                    decA, b_decA = Rb.f32("decA", [128, nt, 8])
                    edecA, b_edecA = Rb.f32("edecA", [128, nt, 8])
                    dtdecA, b_dtdecA = Rb.f32("dtdecA", [128, nt, 8])
                    pcA, bpcA = psum()

                    def csfn(e, pcA=pcA, daa=daa, Ucs=Ucs, Otot=Otot, nt=nt):
                        r = None
                        for t in range(nt):
                            e.matmul(pcA[:, t * 16:t * 16 + 8], lhsT=Ucs, rhs=daa[:, t, :], start=True, stop=True)
                            r = e.matmul(pcA[:, t * 16 + 8:t * 16 + 16], lhsT=Otot, rhs=daa[:, t, :], start=True, stop=True)
                        return r
                    S.op("pe", csfn, reads=[b_c32, b_daa], writes=[bpcA])
                    cp(csA, pcA[:, 0:nt * 16].rearrange("p (t c) -> p t c", c=16), [bpcA], [b_csA], eng="act")
                    act(ecsA, csA[:, :, 0:8], AF.Exp, [b_csA], [b_ecsA])
                    tt(decA, csA[:, :, 8:16], csA[:, :, 0:8], ALU.subtract, [b_csA], [b_decA])
                    act(decA, decA, AF.Exp, [b_decA], [b_decA])
                    act(edecA, csA[:, :, 8:16], AF.Exp, [b_csA], [b_edecA])
                    tt(dtdecA, dtt, decA, ALU.mult, [b_dtt, b_decA], [b_dtdecA])

                def tile_stats(t, H):
                    if batched:
                        return (ecsA[:, t, :], b_ecsA, edecA[:, t, :], b_edecA, dtdecA[:, t, :], b_dtdecA)
                    e_, be_ = H["ecs"]
                    d_, bd_ = H["edec"]
                    return (e_, be_, d_, bd_, dtdec, b_dtdec)

                def stageA(t):
                    H = hand[t % len(hand)]
                    ecs_, b_ecs_, edec_, b_edec_, dtdec_, b_dtdec_ = tile_stats(t, H)
                    xs32_, b_xs32_ = H["xs32"]
                    Xd_, b_Xd_ = H["Xd"]
                    Xdd_, b_Xdd_ = H["Xdd"]
                    Btok_, b_Btok_ = H["Btok"]
                    MT_, b_MT_ = H["MT"]
                    tc0 = t * 128
                    da_t = daa[:, t, :]
                    if not batched:
                        pc_, bpc_ = psum()
                        mm(pc_[:, 0:8], [(Ucs, da_t)], [b_c32, b_daa], [bpc_])
                        mm(pc_[:, 8:16], [(Otot, da_t)], [b_c32, b_daa], [bpc_])
                        yield
                        cp(cs16, pc_[:, 0:16], [bpc_], [b_cs16], eng="act")
                        yield
                        act(ecs_, cs16[:, 0:8], AF.Exp, [b_cs16], [b_ecs_])
                        yield
                        tt(dec, cs16[:, 8:16], cs16[:, 0:8], ALU.subtract, [b_cs16], [b_dec])
                        yield
                        act(dec, dec, AF.Exp, [b_dec], [b_dec])
                        yield
                        act(edec_, cs16[:, 8:16], AF.Exp, [b_cs16], [b_edec_])
                        yield
                        tt(dtdec, dtt[:, t, :], dec, ALU.mult, [b_dtt, b_dec], [b_dtdec])
                        yield
                    px, bpx = psum("mm")
                    transposes([(px[:, j * 128:(j + 1) * 128], xsT[:, j, tc0:tc0 + 128], ident32) for j in range(4)],
                               [b_xsT, b_c32], [bpx])
                    yield
                    cp(xs32_, px[:, :], [bpx], [b_xs32_], eng="act")
                    yield
                    perhead("dve", Xd_, px, dtt[:, t, :], ALU.mult, [bpx, b_dtt], [b_Xd_])
                    yield
                    perhead("act", Xdd_, px, dtdec_, ALU.mult, [bpx, b_dtdec_], [b_Xdd_])
                    yield
                    pb_, bpb_ = psum()
                    pbb = pb_[:, 0:64].bitcast(BF16)
                    transposes([(pbb, BT[:, tc0:tc0 + 128], identb)], [b_BT, b_cbf], [bpb_])
                    yield
                    cp(Btok_, pbb, [bpb_], [b_Btok_], eng="act")
                    yield
                    pcb, bpcb = psum()
                    mm(pcb[:, 0:128], [(BT[:, tc0:tc0 + 128], CT[:, tc0:tc0 + 128])], [b_BT, b_CT], [bpcb])
                    yield
                    tt(CBm, pcb[:, 0:128], Ucs, ALU.mult, [bpcb, b_c32], [b_CBm])
                    yield

                    tt(R1, bcast_mid(c32[:, C_U, :], 8), da_t.unsqueeze(2).to_broadcast([128, 8, 128]), ALU.mult,
                       [b_c32, b_daa], [b_R1])
                    yield

                    def r2fn(e, da_t=da_t, R2=R2):
                        r = None
                        for h in range(8):
                            r = e.activation(out=R2[:, h, :], in_=c32[:, C_CM, :], func=AF.Identity,
                                             bias=da_t[:, h:h + 1], scale=1.0)
                        return r
                    S.op("act", r2fn, reads=[b_c32, b_daa], writes=[b_R2])
                    yield
                    for hb in range(2):
                        pl, bpl = psum("mm")
                        mm(pl[:, :], [(onesb, R1.rearrange("p h l -> p (h l)")[:, hb * 512:(hb + 1) * 512]),
                                      (cbf[:, C_NEGU, :], R2.rearrange("p h l -> p (h l)")[:, hb * 512:(hb + 1) * 512])],
                           [b_cbf, b_R1, b_R2], [bpl])
                        yield
                        act(Lt[:, hb * 4:hb * 4 + 4, :], pl[:, :].rearrange("p (h l) -> p h l", l=128), AF.Exp,
                            [bpl], [b_Lt])
                        yield
                    tt(MT_, Lt, bcast_mid(CBm, 8), ALU.mult, [b_Lt, b_CBm], [b_MT_])
                    yield

                def stageB(t):
                    H = hand[t % len(hand)]
                    ecs_, b_ecs_, edec_, b_edec_, dtdec_, b_dtdec_ = tile_stats(t, H)
                    xs32_, b_xs32_ = H["xs32"]
                    Xd_, b_Xd_ = H["Xd"]
                    Xdd_, b_Xdd_ = H["Xdd"]
                    Btok_, b_Btok_ = H["Btok"]
                    MT_, b_MT_ = H["MT"]
                    tc0 = t * 128
                    da_t = daa[:, t, :]
                    py, bpy = psum("long0")

                    def ydiag(e, py=py, MT_=MT_, Xd_=Xd_):
                        r = None
                        for h in range(8):
                            r = e.matmul(py[:, h * 64:(h + 1) * 64], lhsT=MT_[:, h, :], rhs=Xd_[:, h * 64:(h + 1) * 64],
                                         start=True, stop=True)
                        return r
                    S.op("pe", ydiag, reads=[b_MT_, b_Xd_], writes=[bpy])
                    yield
                    pyo, bpyo = psum("long1")
                    if not is_s:
                        mm(pyo[:, :], [(CT[:, tc0:tc0 + 128], prevTb[:, sg * 512:(sg + 1) * 512])],
                           [b_CT, b_prevb], [bpyo])
                        yield
                        pst, bpst = psum("mm")
                        mm(pst[:, :], [(Btok_, Xdd_)], [b_Btok_, b_Xdd_], [bpst])
                        yield
                        pv = prevT32[:, sg * 512:(sg + 1) * 512]
                        perhead("dve", pv, pv, edec_, ALU.mult, [b_prev32, b_edec_, bpst], [b_prev32],
                                in1_2d=pst, op1=ALU.add)
                        yield
                        cp(prevTb[:, sg * 512:(sg + 1) * 512], pv, [b_prev32], [b_prevb], eng="act")
                        yield
                    else:
                        def padfn(e, CTpad=CTpad, CT=CT):
                            r = None
                            for j in range(16):
                                r = e.tensor_copy(out=CTpad[:, j, j * 8:(j + 1) * 8], in_=CT[:, j * 8:(j + 1) * 8])
                            return r
                        S.op("dve", padfn, reads=[b_CT], writes=[b_CTpad])

                        def bmfn(e, Bmask=Bmask, Btok_=Btok_):
                            r = None
                            for j in range(16):
                                r = e.tensor_scalar(out=Bmask[:, j, :], in0=Btok_, scalar1=c32[:, C_IND, j:j + 1],
                                                    scalar2=None, op0=ALU.mult)
                            return r
                        S.op("dve", bmfn, reads=[b_Btok_, b_c32], writes=[b_Bmask])
                        for q in range(4):
                            hp = sg * 4 + q
                            h0, bh0 = nextof(big, "big")
                            h0v = h0[:, :].rearrange("p (j n) -> p j n", n=128)
                            S.dma("sp", h0v, sssm[L][:, 2 * hp:2 * hp + 2].rearrange("j h p n -> (h p) j n"), bh0, True)
                            for qq in range(4):
                                ph, bph = psum("mm")
                                transposes([(ph[:, i * 128:(i + 1) * 128], h0v[:, qq * 4 + i, :], ident32)
                                            for i in range(4)], [bh0, b_c32], [bph])
                                cp(H0T[:, qq * 4:qq * 4 + 4, :], ph[:, :].rearrange("p (j n) -> p j n", n=128),
                                   [bph], [b_H0T], eng=("act" if qq % 2 else "dve"))
                            mm(pyo[:, q * 128:(q + 1) * 128], [(CTpad[:, j, :], H0T[:, j, :]) for j in range(16)],
                               [b_CTpad, b_H0T], [bpyo])

                            def dxfn(e, q=q, da_t=da_t, daexp=daexp):
                                r = None
                                for h2 in range(2):
                                    r = e.activation(out=daexp[:, h2, :], in_=c32[:, C_ONES, 0:64], func=AF.Copy,
                                                     scale=da_t[:, 2 * q + h2:2 * q + h2 + 1])
                                return r
                            S.op("act", dxfn, reads=[b_daa, b_c32], writes=[b_daexp])
                            psc, bpsc = psum()
                            mm(psc[:, 0:16], [(daexp[:, :, :].rearrange("p h d -> p (h d)"), c32[:, C_IND, 0:16])],
                               [b_daexp, b_c32], [bpsc])
                            act(Sc, psc[:, 0:16], AF.Exp, [bpsc], [b_Sc])
                            for half in range(4):
                                pst, bpst = psum("mm")
                                mm(pst[:, :], [(Xdd_[:, q * 128:(q + 1) * 128],
                                                Bmask.rearrange("p j n -> p (j n)")[:, half * 512:(half + 1) * 512])],
                                   [b_Xdd_, b_Bmask], [bpst])
                                ost, bost = nextof(ostage, "ost")
                                ov = ost[:, :].rearrange("p (j n) -> p j n", n=128)

                                def snfn(e, ov=ov, h0v=h0v, half=half, pst=pst, Sc=Sc):
                                    r = None
                                    for jj in range(4):
                                        j = half * 4 + jj
                                        r = e.scalar_tensor_tensor(out=ov[:, jj, :], in0=h0v[:, j, :], scalar=Sc[:, j:j + 1],
                                                                   in1=pst[:, jj * 128:(jj + 1) * 128], op0=ALU.mult,
                                                                   op1=ALU.add)
                                    return r
                                S.op("dve", snfn, reads=[bh0, b_Sc, bpst], writes=[bost])
                                S.dma("sp", ssms[L][half * 4:half * 4 + 4, 2 * hp:2 * hp + 2]
                                      .rearrange("j h p n -> (h p) j n"), ov, bost, False)
                        yield
                    perhead("act", t2, pyo, ecs_, ALU.mult, [bpyo, b_ecs_], [b_t2])
                    yield
                    tt(t1, t2, py[:, :], ALU.add, [b_t2, bpy], [b_t1])
                    yield
                    perhead("dve", t2, xs32_, dsk_bc[:, sg * 8:sg * 8 + 8], ALU.mult, [b_xs32_, b_bcp], [b_t2])
                    yield
                    tt(t1, t1, t2, ALU.add, [b_t1, b_t2], [b_t1])
                    yield
                    tt(t1, t1, zs[:, t, :], ALU.mult, [b_t1, b_zs], [b_t1])
                    yield
                    act(t2, t1, AF.Square, [b_t1], [b_t2])
                    yield
                    S.op("dve", lambda e, ss2=ss2, t2=t2: e.reduce_sum(out=ss2[:, 0:1], in_=t2, axis=mybir.AxisListType.X),
                         reads=[b_t2], writes=[b_ss2])
                    yield
                    ts(ss2[:, 1:2], ss2[:, 0:1], 1.0 / 512, EPS, ALU.mult, ALU.add, [b_ss2], [b_ss2])
                    yield
                    act(ss2[:, 1:2], ss2[:, 1:2], AF.Sqrt, [b_ss2], [b_ss2])
                    yield
                    S.op("dve", lambda e, ss2=ss2: e.reciprocal(out=ss2[:, 1:2], in_=ss2[:, 1:2]), reads=[b_ss2], writes=[b_ss2])
                    yield
                    ts(t1, t1, ss2[:, 1:2], None, ALU.mult, None, [b_t1, b_ss2], [b_t1])
                    yield
                    pyt, bpyt = psum("mm")
                    transposes([(pyt[:, j * 128:(j + 1) * 128], t1[:, j * 128:(j + 1) * 128], ident32) for j in range(4)],
                               [b_t1, b_c32], [bpyt])
                    yield

                    tt(yaT[:, sg * 4:sg * 4 + 4, tc0:tc0 + 128], pyt[:, :].rearrange("p (c t) -> p c t", t=128),
                       pcol[:, L, PC_SN + sg * 4:PC_SN + sg * 4 + 4].unsqueeze(2).to_broadcast([128, 4, 128]), ALU.mult,
                       [bpyt, b_pcol], [b_yaT])
                    yield

                def drain(g):
                    for _ in g:
                        pass

                if nt == 1 or not PIPELINE:
                    for t in range(nt):
                        drain(stageA(t))
                        drain(stageB(t))
                else:
                    drain(stageA(0))
                    for t in range(nt):
                        gb = stageB(t)
                        ga = stageA(t + 1) if t + 1 < nt else iter(())
                        while True:
                            a_end = next(ga, "END") == "END"
                            b_end = next(gb, "END") == "END"
                            if a_end and b_end:
                                break
                if last_prompt:
                    pf, bpf = psum()
                    transposes([(pf[:, q * 128:(q + 1) * 128], prevT32[:, sg * 512 + q * 128:sg * 512 + (q + 1) * 128],
                                 ident32) for q in range(4)], [b_prev32, b_c32], [bpf])
                    ost, bost = nextof(ostage, "ost")
                    cp(ost[:, :], pf[:, :], [bpf], [bost])
                    S.dma("sp", ssmp[L].rearrange("(q r) n -> r q n", r=128)[:, sg * 4:(sg + 1) * 4, :],
                          ost[:, :].rearrange("p (q n) -> p q n", n=128), bost, False)

            if STAGES < 3:
                continue
            R.reset()
            uT, b_uT = R.f32("uT", [128, 4, NT])
            vnb, b_vnb = R.bf("vnb", [128, nt, 512])
            g1, b_g1 = R.f32("g1", [128, 512])
            g2, b_g2 = R.f32("g2", [128, 512])
            v32, b_v32 = R.f32("v32", [128, 512])
            st2, b_st2 = R.f32("st2", [128, 4])
            xcb, b_xcb = R.bf("xcb", [128, nt, 512])
            dT, b_dT = R.bf("dT", [128, 4, NT])

            def gelu_from(p_ap, bp_, out_ap, b_out, n):
                if FAST_GELU:
                    act(out_ap, p_ap, AF.Gelu_apprx_tanh, [bp_], [b_out])
                    return
                a, c_ = g1[:, 0:n], g2[:, 0:n]
                act(a, p_ap, AF.Square, [bp_], [b_g1])
                ts(a, a, 0.044715, 1.0, ALU.mult, ALU.add, [b_g1], [b_g1])
                tt(a, a, p_ap, ALU.mult, [b_g1, bp_], [b_g1])
                act(c_, a, AF.Sigmoid, [b_g1], [b_g2], scale=2.0 * math.sqrt(2.0 / math.pi))
                tt(out_ap, c_, p_ap, ALU.mult, [b_g2, bp_], [b_out])

            vu, bwu = wload([(w_in, O_UV, 384)])
            vu2, bwu2 = wload([(w_in, O_UV + 384, 128)])
            for j in range(4):
                p_, bp_ = psum("mm")
                vv, bb, jj = (vu, bwu, j) if j < 3 else (vu2, bwu2, 0)
                mm(p_[:, 0:NT], [(vv[:, kc, jj * 128:(jj + 1) * 128], hT[:, kc, 0:NT]) for kc in range(8)],
                   [bb, b_hT], [bp_])
                gelu_from(p_[:, 0:NT], bp_, uT[:, j, :], b_uT, NT)
            vv1, bwv1 = wload([(w_in, O_UV + 512, 384)])
            vv2, bwv2 = wload([(w_in, O_UV + 896, 128)])
            vset = [(v32, b_v32, g1, b_g1, st2, b_st2)]
            if nt > 1:
                v32b, b_v32b = R.f32("v32b", [128, 512])
                g1b, b_g1b = R.f32("g1b", [128, 512])
                st2b, b_st2b = R.f32("st2b", [128, 4])
                vset.append((v32b, b_v32b, g1b, b_g1b, st2b, b_st2b))

            def vstage(t):
                v32_, b_v32_, g1_, b_g1_, st2_, b_st2_ = vset[t % len(vset)]
                p_, bp_ = psum("mm")
                mm(p_[:, 0:384], [(hT[:, kc, t * 128:(t + 1) * 128], vv1[:, kc, :]) for kc in range(8)],
                   [bwv1, b_hT], [bp_])
                mm(p_[:, 384:512], [(hT[:, kc, t * 128:(t + 1) * 128], vv2[:, kc, :]) for kc in range(8)],
                   [bwv2, b_hT], [bp_])
                yield
                if FAST_GELU:
                    act(v32_, p_[:, :], AF.Gelu_apprx_tanh, [bp_], [b_v32_])
                else:
                    gelu_from(p_[:, :], bp_, v32_, b_v32_, 512)
                yield
                S.op("dve", lambda e, st2_=st2_, v32_=v32_: e.reduce_sum(out=st2_[:, 0:1], in_=v32_, axis=mybir.AxisListType.X),
                     reads=[b_v32_], writes=[b_st2_])
                yield
                ts(st2_[:, 1:2], st2_[:, 0:1], -1.0 / 512, None, ALU.mult, None, [b_st2_], [b_st2_])
                yield
                ts(v32_, v32_, st2_[:, 1:2], None, ALU.add, None, [b_v32_, b_st2_], [b_v32_])
                yield
                act(g1_, v32_, AF.Square, [b_v32_], [b_g1_])
                yield
                S.op("dve", lambda e, st2_=st2_, g1_=g1_: e.reduce_sum(out=st2_[:, 2:3], in_=g1_, axis=mybir.AxisListType.X),
                     reads=[b_g1_], writes=[b_st2_])
                yield
                ts(st2_[:, 3:4], st2_[:, 2:3], 1.0 / 512, EPS, ALU.mult, ALU.add, [b_st2_], [b_st2_])
                yield
                act(st2_[:, 3:4], st2_[:, 3:4], AF.Sqrt, [b_st2_], [b_st2_])
                yield
                S.op("dve", lambda e, st2_=st2_: e.reciprocal(out=st2_[:, 3:4], in_=st2_[:, 3:4]), reads=[b_st2_], writes=[b_st2_])
                yield
                stt(v32_, v32_, st2_[:, 3:4], lng_bc, ALU.mult, ALU.mult, [b_v32_, b_st2_, b_bcp], [b_v32_])
                yield
                want_out = is_s or (tile0 + t == 15)
                if want_out:
                    ost, bost = nextof(ostage, "ost")
                    tt(ost[:, :], v32_, lnb_bc, ALU.add, [b_v32_, b_bcp], [bost])
                    S.dma("sp", (vs if is_s else vp)[L], ost[:, :], bost, False)
                    yield
                    cp(vnb[:, t, :], ost[:, :], [bost], [b_vnb], eng="act")
                else:
                    tt(vnb[:, t, :], v32_, lnb_bc, ALU.add, [b_v32_, b_bcp], [b_vnb])
                yield

            for t0_ in range(0, nt, 2):
                gens = [vstage(t) for t in range(t0_, min(t0_ + 2, nt))]
                live = list(gens)
                while live:
                    for g_ in list(live):
                        if next(g_, "END") == "END":
                            live.remove(g_)
            wm_use, b_wm_use = (wmTs, b_wmTs) if is_s else (wmT, b_wmT)
            for g in range(4):
                p_, bp_ = psum()

                def smm(e, p_=p_, g=g, vnb=vnb, wm_use=wm_use, nt=nt):
                    r = None
                    for t in range(nt):
                        r = e.matmul(p_[:, t * 128:(t + 1) * 128], lhsT=vnb[:, t, g * 128:(g + 1) * 128],
                                     rhs=wm_use[:, g, :], start=True, stop=True)
                    return r
                S.op("pe", smm, reads=[b_vnb, b_wm_use], writes=[bp_])
                if is_s:
                    bias_v = bsp_bc[:, g * 128:g * 128 + 8].unsqueeze(1).to_broadcast([128, 16, 8])
                    o3 = g1[:, 0:NT].rearrange("p (j t) -> p j t", t=8)
                    i3 = p_[:, 0:NT].rearrange("p (j t) -> p j t", t=8)
                else:
                    bias_v = bcast_mid(bsp_bc[:, g * 128:(g + 1) * 128], nt)
                    o3 = g1[:, 0:NT].rearrange("p (a t) -> p a t", t=128)
                    i3 = p_[:, 0:NT].rearrange("p (a t) -> p a t", t=128)
                tt(o3, i3, bias_v, ALU.add, [bp_, b_bcp], [b_g1])
                tt(ybT[:, g, 0:NT], g1[:, 0:NT], uT[:, g, :], ALU.mult, [b_g1, b_uT], [b_ybT])
            vpx1, bwp1 = wload([(w_in, O_POOL, 384)])
            vpx2, bwp2 = wload([(w_in, O_POOL + 384, 128)])
            if is_s:
                S.dma("pool", spAb[:], spool[L][0:120, :], b_spAb, True)
                S.dma("pool", spBb[:], spool[L][120:240, :], b_spBb, True)
                S.dma("sp", pools[L][:, 0:7, :], spool[L].rearrange("(j r) c -> j r c", r=15)[:, 8:15, :], b_d2d, False)
            for t in range(nt):
                p_, bp_ = psum("mm")
                mm(p_[:, 0:384], [(hT[:, kc, t * 128:(t + 1) * 128], vpx1[:, kc, :]) for kc in range(8)],
                   [bwp1, b_hT], [bp_])
                mm(p_[:, 384:512], [(hT[:, kc, t * 128:(t + 1) * 128], vpx2[:, kc, :]) for kc in range(8)],
                   [bwp2, b_hT], [bp_])
                cp(xcb[:, t, :], p_[:, :], [bp_], [b_xcb], eng="act")
                if is_s or (tile0 + t == 15):
                    ost, bost = nextof(ostage, "ost")
                    cp(ost[:, :], p_[:, :], [bp_], [bost])
                    if is_s:
                        for j in range(16):
                            S.dma("sp", pools[L][j, 7:15, :], ost[j * 8:(j + 1) * 8, :], bost, False)
                    else:
                        S.dma("sp", poolp[L], ost[113:128, :], bost, False)
            for g in range(4):
                p_, bp_ = psum()

                def dmm(e, p_=p_, g=g, xcb=xcb, is_s=is_s, nt=nt, tile0=tile0):
                    r = None
                    for t in range(nt):
                        gt = tile0 + t
                        o = p_[:, t * 128:(t + 1) * 128]
                        cur = xcb[:, t, g * 128:(g + 1) * 128]
                        if is_s:
                            e.matmul(o, lhsT=cur, rhs=cbf[:, C_DCURS + g, :], start=True, stop=False)
                            e.matmul(o, lhsT=spAb[:, g * 128:(g + 1) * 128], rhs=cbf[0:120, C_DPA + g, :],
                                     start=False, stop=False)
                            r = e.matmul(o, lhsT=spBb[:, g * 128:(g + 1) * 128], rhs=cbf[0:120, C_DPB + g, :],
                                         start=False, stop=True)
                        else:
                            dc = C_DCUR0 if gt == 0 else C_DCUR
                            prev = xcprev[:, g * 128:(g + 1) * 128] if t == 0 else xcb[:, t - 1, g * 128:(g + 1) * 128]
                            e.matmul(o, lhsT=cur, rhs=cbf[:, dc + g, :], start=True, stop=False)
                            r = e.matmul(o, lhsT=prev, rhs=cbf[:, C_DPREV + g, :], start=False, stop=True)
                    return r
                S.op("pe", dmm, reads=[b_xcb, b_xcprev, b_cbf, b_spAb, b_spBb], writes=[bp_])
                cp(dT[:, g, :], p_[:, 0:NT], [bp_], [b_dT], eng=("act" if g % 2 else "dve"))
            if not is_s:
                cp(xcprev[:], xcb[:, nt - 1, :], [b_xcb], [b_xcprev])
            t_pw, b_pw = nextof(slots, "slot")
            pwv = t_pw[:, 0:512].rearrange("p (g d) -> p g d", d=128)
            S.dma("pool", pwv, W["pool_w"][L].rearrange("g c d -> c g d"), b_pw, True)
            for g in range(4):
                p_, bp_ = psum("mm")
                mm(p_[:, 0:NT], [(pwv[:, g, :], dT[:, g, :])], [b_pw, b_dT], [bp_])
                act(ycT[:, g, 0:NT], p_[:, 0:NT], AF.Copy, [bp_, b_pcol], [b_ycT], scale=pc(PC_PS + g))

            if STAGES < 4:
                continue
            R.reset()
            mgT, b_mgT = R.bf("mgT", [128, 8, NT])
            sgt, b_sgt = R.f32("sgt", [128, NT])
            acc, b_acc = R.f32("acc", [128, NT])
            acc2, b_acc2 = R.f32("acc2", [128, NT])
            for j in range(8):
                first = True
                for bi, (wbr, yT, b_y, KCb) in enumerate(((W["w_br_a"][L], yaT, b_yaT, 8),
                                                          (W["w_br_b"][L], ybT, b_ybT, 4),
                                                          (W["w_br_c"][L], ycT, b_ycT, 4))):
                    (vg, vw), bw = wload2([(w_in, O_GATE + bi * 1024 + j * 128, 128), (wbr, j * 128, 128)])
                    pg, bpg = psum("mm")
                    mm(pg[:, 0:NT], [(vg[:, kc, :], hT[:, kc, 0:NT]) for kc in range(8)], [bw, b_hT], [bpg])
                    pb2, bpb2 = psum("mm")
                    mm(pb2[:, 0:NT], [(vw[:, kc, :], yT[:, kc, 0:NT]) for kc in range(KCb)], [bw, b_y], [bpb2])
                    act(sgt, pg[:, 0:NT], AF.Sigmoid, [bpg], [b_sgt])
                    if first:
                        tt(acc, sgt, pb2[:, 0:NT], ALU.mult, [b_sgt, bpb2], [b_acc])
                        first = False
                    else:
                        tt(acc2, sgt, pb2[:, 0:NT], ALU.mult, [b_sgt, bpb2], [b_acc2])
                        tt(acc, acc, acc2, ALU.add, [b_acc, b_acc2], [b_acc])
                cp(mgT[:, j, :], acc, [b_acc], [b_mgT], eng="act")

            def add_x(j, p_, bp_):
                tt(xT[:, j, tok0:tok0 + NT], xT[:, j, tok0:tok0 + NT], p_[:, 0:NT], ALU.add, [b_x, bp_], [b_x])
            for jp in range(3):
                ncb = 3 if jp < 2 else 2
                view, bw = wload([(W["w_out"][L], jp * 384, ncb * 128)])
                for jj in range(ncb):
                    j = jp * 3 + jj
                    p_, bp_ = psum("mm")
                    mm(p_[:, 0:NT], [(view[:, kc, jj * 128:(jj + 1) * 128], mgT[:, kc, :]) for kc in range(8)],
                       [bw, b_mgT], [bp_])
                    add_x(j, p_, bp_)

            if STAGES < 5:
                continue
            R.reset()
            rmsnorm_T(tok0, NT, lambda c: pc(PC_NF + c), hT, b_hT)
            actT, b_actT = R.bf("actT", [128, 22, NT])
            sg2, b_sg2 = R.f32("sg2", [128, NT])
            wgu = W["w_gate_up"][L]
            for fc in range(22):
                view, bw = wload([(wgu, fc * 128, 128), (wgu, D_FF + fc * 128, 128)])
                pg, bpg = psum("mm")
                mm(pg[:, 0:NT], [(view[:, kc, 0:128], hT[:, kc, 0:NT]) for kc in range(8)], [bw, b_hT], [bpg])
                pu, bpu = psum("mm")
                mm(pu[:, 0:NT], [(view[:, kc, 128:256], hT[:, kc, 0:NT]) for kc in range(8)], [bw, b_hT], [bpu])
                act(sg2, pg[:, 0:NT], AF.Silu, [bpg], [b_sg2])
                tt(actT[:, fc, :], sg2, pu[:, 0:NT], ALU.mult, [b_sg2, bpu], [b_actT])
            for j in range(8):
                view, bw = wload([(W["w_down"][L], j * 128, 128)])
                p_, bp_ = psum("mm")
                mm(p_[:, 0:NT], [(view[:, fc, :], actT[:, fc, :]) for fc in range(22)], [bw, b_actT], [bp_])
                add_x(j, p_, bp_)

            if STAGES < 6:
                continue
            R.reset()
            rmsnorm_T(tok0, NT, lambda c: pc(PC_NP + c), hT, b_hT)
            pT, b_pT = R.bf("pT", [128, 2, NT])
            sg3, b_sg3 = R.f32("sg3", [128, NT])
            for t in range(nt):
                pst_, bpst_ = nextof(pstage, "pst")
                S.dma("sp", pst_[:, :], pin[L][tok0 + t * 128:tok0 + (t + 1) * 128, :], bpst_, True)
                p_, bp_ = psum()
                transposes([(p_[:, i * 128:(i + 1) * 128], pst_[:, i * 128:(i + 1) * 128], ident32) for i in range(2)],
                           [bpst_, b_c32], [bp_])
                cp(pT[:, :, t * 128:(t + 1) * 128], p_[:, 0:256].rearrange("p (a b) -> p a b", b=128), [bp_], [b_pT])
            for jp in range(4):
                vg, bg = wload([(W["w_ple_gate"][L], jp * 256, 256)])
                vu_, bu_ = wload([(W["w_ple_up"][L], jp * 256, 256)])
                for jj in range(2):
                    j = jp * 2 + jj
                    cs_ = slice(jj * 128, (jj + 1) * 128)
                    pg, bpg = psum("mm")
                    mm(pg[:, 0:NT], [(vg[:, kc, cs_], hT[:, kc, 0:NT]) for kc in range(8)], [bg, b_hT], [bpg])
                    pu, bpu = psum("mm")
                    mm(pu[:, 0:NT], [(vu_[:, kc, cs_], pT[:, kc, :]) for kc in range(2)], [bu_, b_pT], [bpu])
                    act(sg3, pg[:, 0:NT], AF.Sigmoid, [bpg], [b_sg3])
                    tt(sg3, sg3, pu[:, 0:NT], ALU.mult, [b_sg3, bpu], [b_sg3])
                    tt(xT[:, j, tok0:tok0 + NT], xT[:, j, tok0:tok0 + NT], sg3, ALU.add, [b_x, b_sg3], [b_x])

    for (tile0, nt, is_s) in GROUPS:
        NT = nt * 128
        tok0 = tile0 * 128
        R.reset()
        yT, b_yT = R.f32("yT", [128, 8, NT])
        rmsnorm_T(tok0, NT, lambda c: fcol[:, c:c + 1], yT, b_yT)
        for t in range(nt):
            st, bst = nextof(big, "big")
            for hb in range(2):
                p_, bp_ = psum()
                transposes([(p_[:, j * 128:(j + 1) * 128], yT[:, hb * 4 + j, t * 128:(t + 1) * 128], ident32)
                            for j in range(4)], [b_yT, b_c32], [bp_])
                cp(st[:, hb * 512:(hb + 1) * 512], p_[:, :], [bp_], [bst], eng=("act" if hb else "dve"))
            S.dma("sp", yout[tok0 + t * 128:tok0 + (t + 1) * 128, :], st[:, 0:1024], bst, False)

    S.emit()


_CACHE = {}


def _get_program():
    if "nc" not in _CACHE:
        _CACHE["nc"] = build_program()
    return _CACHE["nc"]


def make_in_maps(inputs):
    f = lambda a: np.ascontiguousarray(np.asarray(a, dtype=np.float32))
    cst = make_consts()
    wnames = ["norm_mix", "w_in", "conv_w", "conv_b", "dt_bias", "a_log", "d_skip", "ssd_norm", "sgu_ln_g",
              "sgu_ln_b", "w_spatial", "b_spatial", "pool_w", "pool_scale", "w_br_a", "w_br_b", "w_br_c", "w_out",
              "norm_ffn", "w_gate_up", "w_down", "norm_ple", "w_ple_gate", "w_ple_up", "final_norm"]
    wts = {n: f(inputs[n]) for n in wnames}
    xp, xs = f(inputs["x_prompt"]), f(inputs["x_sample"])
    pp, psm = f(inputs["p_prompt"]), f(inputs["p_sample"])
    sc, ss, spl = f(inputs["state_conv"]), f(inputs["state_ssm"]), f(inputs["state_pool"])
    maps = []
    for b in range(NCORES):
        sl = slice(16 * b, 16 * b + 16)
        m = dict(wts)
        m["xin"] = np.ascontiguousarray(np.concatenate([xp[b], xs[sl].reshape(128, D)], axis=0))
        m["pin"] = np.ascontiguousarray(np.concatenate([pp[:, b], psm[:, sl].reshape(DEPTH, 128, 256)], axis=1))
        m["sconv"] = np.ascontiguousarray(sc[:, sl].reshape(DEPTH, 48, 1536))
        m["sssm"] = np.ascontiguousarray(ss[:, sl])
        m["spool"] = np.ascontiguousarray(spl[:, sl].reshape(DEPTH, 240, 512))
        m["cst"] = cst
        maps.append(m)
    return maps


def assemble(results):
    B = NCORES
    y_p = np.stack([r["yout"][:2048] for r in results]).astype(np.float32)
    y_s = np.concatenate([r["yout"][2048:].reshape(16, 8, D) for r in results], axis=0).astype(np.float32)
    conv_p = np.stack([r["convp"] for r in results], axis=1)
    ssm_p = np.stack([r["ssmp"].reshape(DEPTH, 16, 64, 128) for r in results], axis=1)
    pool_p = np.stack([r["poolp"] for r in results], axis=1)
    v_p = np.stack([r["vp"] for r in results], axis=1)
    conv_s = np.concatenate([r["convs"].reshape(DEPTH, 16, 3, 1536) for r in results], axis=1)
    ssm_s = np.concatenate([r["ssms"] for r in results], axis=1)
    pool_s = np.concatenate([r["pools"] for r in results], axis=1)
    v_s = np.concatenate([r["vs"].reshape(DEPTH, 16, 8, 512) for r in results], axis=1)
    outs = (y_p, y_s, conv_p, ssm_p, pool_p, v_p, conv_s, ssm_s, pool_s, v_s)
    return tuple(np.ascontiguousarray(o, dtype=np.float32) for o in outs)


def kernel(**inputs):
    nc = _get_program()
    maps = make_in_maps(inputs)
    if DEBUG_CORES:
        res = run_bass_kernel_spmd(nc, maps[:DEBUG_CORES], core_ids=list(range(DEBUG_CORES)), trace=DEBUG_TRACE)
        print("DEBUG exec_time_ns", res.exec_time_ns, flush=True)
        rs = list(res.results)
        return assemble(rs + [rs[0]] * (NCORES - len(rs)))
    res = run_bass_kernel_spmd(nc, maps, core_ids=list(range(NCORES)))
    return assemble(res.results)
```
